# Optimizing a Trainium2 kernel written in Bass

```python
import math
import jax, jax.numpy as jnp
from jax import lax
import numpy as np

D_MODEL = 1024
BATCH = 32
SEQ = 2048
DEPTH = 1

N_META = 16
HEAD_DIM = 64
N_Q_HEADS = D_MODEL // HEAD_DIM
N_KV_HEADS = N_Q_HEADS // 4
Q_PER_KV = N_Q_HEADS // N_KV_HEADS
WINDOW = 128
BLOCK = 128
ATTN_WIDTH = N_Q_HEADS * HEAD_DIM
KV_WIDTH = N_KV_HEADS * HEAD_DIM
SSM_GROUP = 16
SSM_WIDTH = D_MODEL // 2
SSM_GROUPS = SSM_WIDTH // SSM_GROUP
SSM_STATE = 64
D_FF = ((8 * D_MODEL // 3 + 255) // 256) * 256
NORM_EPS = 1e-6
NEG_INF = -1e30
SPLITS = [ATTN_WIDTH, ATTN_WIDTH + KV_WIDTH, ATTN_WIDTH + 2 * KV_WIDTH,
          ATTN_WIDTH + 2 * KV_WIDTH + SSM_WIDTH,
          ATTN_WIDTH + 2 * KV_WIDTH + SSM_WIDTH + D_MODEL]
IN_WIDTH = SPLITS[-1] + D_MODEL

kernel_name = "hybrid_swa_s5_gated_macaron"


def rmsnorm(x, g):
    xf = x.astype(jnp.float32)
    y = xf * lax.rsqrt(jnp.mean(xf * xf, axis=-1, keepdims=True) + NORM_EPS)
    return (y * g.astype(jnp.float32)).astype(x.dtype)


def swiglu(h, w1, w3, w2):
    return (jax.nn.silu(h @ w1) * (h @ w3)) @ w2


def sink_softmax(scores, mask, sink):
    scores = jnp.where(mask, scores, NEG_INF)
    sink_b = jnp.broadcast_to(sink[:, :, None, None], scores.shape[:-1] + (1,))
    p = jax.nn.softmax(jnp.concatenate([scores, sink_b], axis=-1), axis=-1)
    return p[..., :-1]


def sliding_window_attention(q, k, v, sinks):
    b, l = q.shape[0], q.shape[1]
    s = l - N_META
    nb = s // BLOCK
    q = (q * (HEAD_DIM ** -0.5)).reshape(b, l, N_KV_HEADS, Q_PER_KV, HEAD_DIM)
    k = k.reshape(b, l, N_KV_HEADS, HEAD_DIM)
    v = v.reshape(b, l, N_KV_HEADS, HEAD_DIM)
    sink = sinks.astype(jnp.float32).reshape(N_KV_HEADS, Q_PER_KV)
    q_meta, q_real = q[:, :N_META], q[:, N_META:]
    k_meta, k_real = k[:, :N_META], k[:, N_META:]
    v_meta, v_real = v[:, :N_META], v[:, N_META:]

    sc_m = jnp.einsum('bqkgd,bskd->bkgqs', q_meta, k_meta).astype(jnp.float32)
    mask_m = jnp.tril(jnp.ones((N_META, N_META), dtype=bool))
    p_m = sink_softmax(sc_m, mask_m, sink)
    out_m = jnp.einsum('bkgqs,bskd->bqkgd', p_m.astype(v.dtype), v_meta)

    qb = q_real.reshape(b, nb, BLOCK, N_KV_HEADS, Q_PER_KV, HEAD_DIM)
    kb = k_real.reshape(b, nb, BLOCK, N_KV_HEADS, HEAD_DIM)
    vb = v_real.reshape(b, nb, BLOCK, N_KV_HEADS, HEAD_DIM)
    k_band = jnp.concatenate([jnp.concatenate([jnp.zeros_like(kb[:, :1]), kb[:, :-1]], axis=1), kb], axis=2)
    v_band = jnp.concatenate([jnp.concatenate([jnp.zeros_like(vb[:, :1]), vb[:, :-1]], axis=1), vb], axis=2)
    qi = jnp.arange(BLOCK)[:, None]
    kj = jnp.arange(2 * BLOCK)[None, :]
    rel = BLOCK + qi - kj
    band_ok = (rel >= 0) & (rel < WINDOW)
    meta_ok = jnp.ones((BLOCK, N_META), dtype=bool)

    def block_fn(args):
        n, qn, kn, vn = args
        keys = jnp.concatenate([k_meta, kn], axis=1)
        vals = jnp.concatenate([v_meta, vn], axis=1)
        sc = jnp.einsum('bqkgd,bskd->bkgqs', qn, keys).astype(jnp.float32)
        valid = band_ok & ((n - 1) * BLOCK + kj >= 0)
        mask = jnp.concatenate([meta_ok, valid], axis=1)
        p = sink_softmax(sc, mask, sink)
        return jnp.einsum('bkgqs,bskd->bqkgd', p.astype(vals.dtype), vals)

    out_r = lax.map(block_fn, (jnp.arange(nb, dtype=jnp.int32), jnp.moveaxis(qb, 1, 0),
                               jnp.moveaxis(k_band, 1, 0), jnp.moveaxis(v_band, 1, 0)))
    out_r = jnp.moveaxis(out_r, 0, 1).reshape(b, s, ATTN_WIDTH)
    return jnp.concatenate([out_m.reshape(b, N_META, ATTN_WIDTH), out_r], axis=1)


def s5_ssm(u, a_re, a_im, log_step, b_re, b_im, c_re, c_im, d_skip):
    b, l, _ = u.shape
    uf = u.astype(jnp.float32).reshape(b, l, SSM_GROUPS, SSM_GROUP)
    ar, ai = a_re.astype(jnp.float32), a_im.astype(jnp.float32)
    step = jnp.exp(log_step.astype(jnp.float32))[:, None]
    mag = jnp.exp(ar * step)
    ang = ai * step
    lam_re, lam_im = mag * jnp.cos(ang), mag * jnp.sin(ang)
    den = ar * ar + ai * ai
    nr, ni = lam_re - 1.0, lam_im
    coef_re = (nr * ar + ni * ai) / den
    coef_im = (ni * ar - nr * ai) / den
    br, bi = b_re.astype(jnp.float32), b_im.astype(jnp.float32)
    bb_re = coef_re[..., None] * br - coef_im[..., None] * bi
    bb_im = coef_re[..., None] * bi + coef_im[..., None] * br
    bu_re = jnp.einsum('blgc,gnc->blgn', uf, bb_re)
    bu_im = jnp.einsum('blgc,gnc->blgn', uf, bb_im)
    la_re = jnp.broadcast_to(lam_re[None, None], (1, l, SSM_GROUPS, SSM_STATE))
    la_im = jnp.broadcast_to(lam_im[None, None], (1, l, SSM_GROUPS, SSM_STATE))

    def combine(e1, e2):
        a1r, a1i, b1r, b1i = e1
        a2r, a2i, b2r, b2i = e2
        return (a2r * a1r - a2i * a1i, a2r * a1i + a2i * a1r,
                a2r * b1r - a2i * b1i + b2r, a2r * b1i + a2i * b1r + b2i)

    _, _, x_re, x_im = lax.associative_scan(combine, (la_re, la_im, bu_re, bu_im), axis=1)
    y = (jnp.einsum('blgn,gcn->blgc', x_re, c_re.astype(jnp.float32))
         - jnp.einsum('blgn,gcn->blgc', x_im, c_im.astype(jnp.float32)))
    y = y + d_skip.astype(jnp.float32).reshape(SSM_GROUPS, SSM_GROUP) * uf
    return y.reshape(b, l, SSM_WIDTH).astype(u.dtype)


def setup_inputs(seed: int = 0) -> dict:
    key = jax.random.key(seed)
    ks = jax.random.split(key, 32)
    nrm = lambda k, shape, scale: jax.random.normal(k, shape, jnp.float32) * scale
    n_idx = jnp.arange(SSM_STATE, dtype=jnp.float32)
    return {
        "x": nrm(ks[0], (BATCH, SEQ, D_MODEL), 1.0),
        "meta_tokens": nrm(ks[1], (N_META, D_MODEL), 1.0),
        "ffn1_norm": 1.0 + nrm(ks[2], (DEPTH, D_MODEL), 0.02),
        "ffn1_w1": nrm(ks[3], (DEPTH, D_MODEL, D_FF), D_MODEL ** -0.5),
        "ffn1_w3": nrm(ks[4], (DEPTH, D_MODEL, D_FF), D_MODEL ** -0.5),
        "ffn1_w2": nrm(ks[5], (DEPTH, D_FF, D_MODEL), D_FF ** -0.5),
        "mix_norm": 1.0 + nrm(ks[6], (DEPTH, D_MODEL), 0.02),
        "w_in": nrm(ks[7], (DEPTH, D_MODEL, IN_WIDTH), D_MODEL ** -0.5),
        "attn_sinks": nrm(ks[8], (DEPTH, N_Q_HEADS), 0.5),
        "ssm_a_re": -0.5 + nrm(ks[9], (DEPTH, SSM_GROUPS, SSM_STATE), 0.01),
        "ssm_a_im": math.pi * n_idx[None, None, :] + nrm(ks[10], (DEPTH, SSM_GROUPS, SSM_STATE), 0.01),
        "ssm_log_step": jax.random.uniform(ks[11], (DEPTH, SSM_GROUPS), jnp.float32,
                                           math.log(0.001), math.log(0.1)),
        "ssm_b_re": nrm(ks[12], (DEPTH, SSM_GROUPS, SSM_STATE, SSM_GROUP), (2 * SSM_GROUP) ** -0.5),
        "ssm_b_im": nrm(ks[13], (DEPTH, SSM_GROUPS, SSM_STATE, SSM_GROUP), (2 * SSM_GROUP) ** -0.5),
        "ssm_c_re": nrm(ks[14], (DEPTH, SSM_GROUPS, SSM_GROUP, SSM_STATE), SSM_STATE ** -0.5),
        "ssm_c_im": nrm(ks[15], (DEPTH, SSM_GROUPS, SSM_GROUP, SSM_STATE), SSM_STATE ** -0.5),
        "ssm_d": nrm(ks[16], (DEPTH, SSM_WIDTH), 1.0),
        "ssm_glu_a": nrm(ks[17], (DEPTH, SSM_WIDTH, D_MODEL), SSM_WIDTH ** -0.5),
        "ssm_glu_b": nrm(ks[18], (DEPTH, SSM_WIDTH, D_MODEL), SSM_WIDTH ** -0.5),
        "w_out": nrm(ks[19], (DEPTH, D_MODEL, D_MODEL), D_MODEL ** -0.5),
        "ffn2_norm": 1.0 + nrm(ks[20], (DEPTH, D_MODEL), 0.02),
        "ffn2_w1": nrm(ks[21], (DEPTH, D_MODEL, D_FF), D_MODEL ** -0.5),
        "ffn2_w3": nrm(ks[22], (DEPTH, D_MODEL, D_FF), D_MODEL ** -0.5),
        "ffn2_w2": nrm(ks[23], (DEPTH, D_FF, D_MODEL), D_FF ** -0.5),
        "final_norm": 1.0 + nrm(ks[24], (D_MODEL,), 0.02),
    }


def reference(x, meta_tokens, ffn1_norm, ffn1_w1, ffn1_w3, ffn1_w2, mix_norm, w_in,
              attn_sinks, ssm_a_re, ssm_a_im, ssm_log_step, ssm_b_re, ssm_b_im,
              ssm_c_re, ssm_c_im, ssm_d, ssm_glu_a, ssm_glu_b, w_out,
              ffn2_norm, ffn2_w1, ffn2_w3, ffn2_w2, final_norm):
    b = x.shape[0]
    meta = jnp.broadcast_to(meta_tokens[None].astype(x.dtype), (b, N_META, D_MODEL))
    h = jnp.concatenate([meta, x], axis=1)
    for i in range(DEPTH):
        h = h + 0.5 * swiglu(rmsnorm(h, ffn1_norm[i]), ffn1_w1[i], ffn1_w3[i], ffn1_w2[i])
        hn = rmsnorm(h, mix_norm[i])
        q, k, v, u, g_attn, g_ssm = jnp.split(hn @ w_in[i], SPLITS, axis=-1)
        attn = sliding_window_attention(q, k, v, attn_sinks[i])
        y = s5_ssm(u, ssm_a_re[i], ssm_a_im[i], ssm_log_step[i], ssm_b_re[i], ssm_b_im[i],
                   ssm_c_re[i], ssm_c_im[i], ssm_d[i])
        y = jax.nn.gelu(y)
        ssm = (y @ ssm_glu_a[i]) * jax.nn.sigmoid(y @ ssm_glu_b[i])
        merged = jax.nn.sigmoid(g_attn) * attn + jax.nn.sigmoid(g_ssm) * ssm
        h = h + merged @ w_out[i]
        h = h + 0.5 * swiglu(rmsnorm(h, ffn2_norm[i]), ffn2_w1[i], ffn2_w3[i], ffn2_w2[i])
    return rmsnorm(h, final_norm)[:, N_META:]
```

```python
import numpy as np
from contextlib import ExitStack
import concourse.bass as bass
import concourse.mybir as mybir
from concourse.bass_utils import run_bass_kernel_spmd

F32 = mybir.dt.float32
BF16 = mybir.dt.bfloat16
AF = mybir.ActivationFunctionType
ALU = mybir.AluOpType

D = 1024
FF = 2816
FT = 22
NS = 4
NEG = -1.0e6
GELU_C = 1.5957691216057308
EPS = 1e-6

U_W13 = (0, 32)
U_W2 = (11, 43)
U_WIN = 17
U_GLU = 26
U_WOUT = 30
NUNIT = 49


class Res:
    __slots__ = ("w", "rs", "name", "excl")

    def __init__(self, name="", excl=False):
        self.w = None
        self.rs = []
        self.name = name
        self.excl = excl


class Op:
    __slots__ = ("eng", "fn", "deps", "needed", "semkey", "count")


class Sched:
    ENGS = ("pe", "act", "dve", "pool", "sp")

    def __init__(self):
        self.q = {e: [] for e in self.ENGS}
        self.dma_counts = {}

    def add(self, eng, fn, reads=(), writes=(), dma=None, ndma=1):
        op = Op()
        op.eng = eng
        op.fn = fn
        op.deps = []
        op.needed = False
        op.semkey = dma
        op.count = 0
        if dma is not None:
            c = self.dma_counts.get(dma, 0) + 16 * ndma
            self.dma_counts[dma] = c
            op.count = c
            op.needed = True
        seen = set()

        def dep(d):
            if d is None or id(d) in seen:
                return
            seen.add(id(d))
            op.deps.append(d)

        for r in reads:
            dep(r.w)
            if r.excl:
                for rd in r.rs:
                    if rd.eng != eng:
                        dep(rd)
        for w in writes:
            dep(w.w)
            for rd in w.rs:
                if rd.eng == eng and rd.semkey is None and dma is None:
                    continue
                dep(rd)
        for r in reads:
            if dma is None:
                r.rs = [x for x in r.rs if x.eng != eng or x.semkey is not None]
            r.rs.append(op)
        for w in writes:
            w.w = op
            w.rs = []
        self.q[eng].append(op)
        return op

    def finalize(self):
        for e in self.ENGS:
            for op in self.q[e]:
                for d in op.deps:
                    if d.semkey is None and d.eng == "pe" and op.eng == "pe" and op.semkey is None:
                        continue
                    d.needed = True
        for e in self.ENGS:
            c = 0
            for op in self.q[e]:
                if op.semkey is None and op.needed:
                    c += 1
                    op.count = c

    def emit(self, e, eng, sems):
        waited = {}
        for op in self.q[e]:
            for d in op.deps:
                if d.semkey is None and d.eng == "pe" and e == "pe" and op.semkey is None:
                    continue
                key = d.semkey if d.semkey is not None else ("E", d.eng)
                if waited.get(key, 0) >= d.count:
                    continue
                eng.wait_ge(sems[key], d.count)
                waited[key] = d.count
            ins = op.fn(eng)
            if op.semkey is not None:
                if not isinstance(ins, (list, tuple)):
                    ins = [ins]
                for i in ins:
                    i.then_inc(sems[op.semkey], 16)
            elif op.needed:
                ins.then_inc(sems[("E", e)], 1)


def build(NSEQ, NTILE, dbg=None, stage=99):
    dbg = dbg or {}
    S = NTILE * 512
    nc = bass.Bass("TRN2", target_bir_lowering=False)
    dram = lambda n, sh, dt, kind="ExternalInput": nc.dram_tensor(n, sh, dt, kind=kind).ap()
    x_d = dram("x", [NSEQ, S, D], F32)
    meta_d = dram("meta", [16, D], F32)
    wts_d = dram("wts", [NUNIT, 128, 4096], F32)
    gam_d = dram("gam", [128, 24], F32)
    gfin_d = dram("gfin", [128, D], F32)
    ssmp_d = dram("ssmp", [128, 3, 16], F32)
    bl_d = dram("bl", [128, 4, 2, 128], F32)
    cl_d = dram("cl", [128, 16, 2, 32], F32)
    dcol_d = dram("dcol", [128, 4], F32)
    sink_d = dram("sinkrow", [1, 2048], F32)
    ident_d = dram("ident", [128, 128], F32)
    maskp_d = dram("maskp", [128, 512], F32)
    maskc_d = dram("maskc", [128, 512], F32)
    oneh_d = dram("onehot", [1, 32], F32)
    y_d = dram("y", [NSEQ, S, D], F32, kind="ExternalOutput")
    wsc_d = dram("wsc", [NUNIT, 128, 4096], BF16, kind="Internal")
    dbg_d = {}
    for name, shape in dbg.items():
        dbg_d[name] = dram("dbg_" + name, list(shape), F32, kind="ExternalOutput")

    sc = Sched()
    es = ExitStack()
    with es:
        def sb(name, shape, dt):
            return es.enter_context(nc.sbuf_tensor("sb_" + name, list(shape), dt))

        h2 = sb("h", [128, 2, 4, D], F32)
        h2_res = [[Res("h%d_%d" % (j, i)) for i in range(4)] for j in range(2)]
        junk_res = Res()
        hn = sb("hn", [128, 2, D], BF16)
        hn_res = [Res(), Res()]
        stat = sb("stat", [128, 16], F32)
        stat_res = [Res() for _ in range(4)]
        eps_t = sb("eps_t", [128, 1], F32)
        eps_res = Res()
        hnT = sb("hnT", [128, 8, 512], BF16)
        hnT_res = Res()
        hnTf = sb("hnTf", [128, 8, 512], BF16)
        hnTf_res = Res()
        wring = sb("wring", [128, NS, 4096], BF16)
        wring_res = [Res("slot%d" % i) for i in range(NS)]
        scrA = sb("scrA", [128, FT * 512], BF16)
        actT = scrA[:, :].rearrange("p (a b) -> p a b", a=FT)
        actT_res = Res()
        mg = scrA[:, 0:4096].bitcast(F32).rearrange("p (a b c) -> p a b c", a=2, b=2)
        sg = scrA[:, 4096:6144].rearrange("p (a b c) -> p a b c", a=2, b=2)
        gl = sb("gl", [128, 1, 2, 512], F32)
        rden = sb("rden", [128, 2, 256], F32)
        sil = sb("sil", [128, 2, 512], BF16)
        sil_res = [Res(), Res()]
        junk = sil[:, :, :].rearrange("p a b -> p (a b)")
        ost_res = [Res(), Res()]
        stg32 = wring[:, 0:4, :].rearrange("p a b -> p (a b)").bitcast(F32).rearrange("p (a b) -> p a b", a=2)
        stg32_res = [Res(), Res()]
        stg16 = scrA[:, 0:8192].rearrange("p (a b) -> p a b", a=2)
        stg16_res = [Res(), Res()]
        ident = sb("ident", [128, 128], BF16)
        maskp = sb("maskp", [128, 512], BF16)
        maskc = sb("maskc", [128, 512], BF16)
        ones = sb("ones", [128, 64], BF16)
        oneh = sb("oneh", [1, 32], BF16)
        sinkb = hn[0:1, :, :].rearrange("p a b -> p (a b)")
        gam = sb("gam", [128, 24], F32)
        gfin = sb("gfin", [128, D], F32)
        dcol = sb("dcol", [128, 4], F32)
        const_res = Res("const")
        ssmp = sb("ssmp", [128, 3, 16], F32)
        sp_t = sb("sp_t", [128, 12, 16], F32)
        Rt = sb("Rt", [128, 16], F32)
        COS = sb("COS", [128, 16, 129], F32)
        SIN = sb("SIN", [128, 16, 129], F32)
        Blb = sb("Blb", [128, 4, 2, 128], BF16)
        Clb = sb("Clb", [128, 16, 3, 32], BF16)
        ST = sb("ST", [128, 16, 2], F32)
        ST0 = sb("ST0", [128, 16, 2], F32)
        ST_res = [Res() for _ in range(16)]
        ST0_res = Res()
        qT = sb("qT", [128, 8, 512], BF16)
        qT_res = Res()
        ostage = qT[:, :, :].rearrange("p a b -> p (a b)").bitcast(F32).rearrange("p (a b) -> p a b", a=2)
        cl32 = qT[:, 0:4, :].rearrange("p a b -> p (a b)").bitcast(F32).rearrange("p (a b c) -> p a b c", a=16, b=2)
        clt = hnT[:, :, :].rearrange("p a b -> p (a b)").bitcast(F32).rearrange("p (a b c) -> p a b c", a=4, b=16)
        kk = sb("kk", [128, 4, 2, 16 + 5 * 128], BF16)
        kk_res = [Res() for _ in range(6)]
        vt = sb("vt", [128, 5, 4, 192], BF16)
        vt_res = [Res() for _ in range(5)]
        vmeta = sb("vmeta", [32, 4, 192], BF16)
        vmeta_res = Res()
        Em = sb("Em", [32, 4, 512], BF16)
        Em_res = [Res() for _ in range(4)]
        Eb = sb("Eb", [128, 4, 512], BF16)
        Eb_res = [Res() for _ in range(4)]
        attnT = sb("attnT", [128, 8, 512], BF16)
        attnT_res = [Res() for _ in range(8)]
        rden_res = [Res(), Res()]
        uTb = sb("uTb", [128, 4, 512], BF16)
        uT_res = Res()
        tcts = sb("tcts", [128, 1024], F32)
        TC = tcts[:, 0:512].rearrange("p (a b c) -> p a b c", a=2, b=2)
        TS = tcts[:, 512:1024].rearrange("p (a b c) -> p a b c", a=2, b=2)
        sbt = tcts[:, 0:1024].rearrange("p (a b) -> p a b", a=2)
        Wt = sb("Wt", [128, 2, 2, 128], F32)
        Zt = sb("Zt", [128, 2, 2, 128], F32)
        QC = sb("QC", [128, 2, 2, 128], BF16)
        QS = sb("QS", [128, 2, 2, 128], BF16)
        zl = sb("zl", [128, 16, 2], F32)
        zl_res = Res()
        ct4 = sb("ct4", [128, 4, 16], F32)
        ct_res = Res()
        ssm_res = {n: [Res(), Res()] for n in ("TC", "TS", "W", "Z", "QC", "QS")}
        yf = sb("yf", [128, 1, 512], F32)
        yf_res = [Res(), Res()]
        gl_res = [Res(), Res()]
        ygT = sb("ygT", [128, 4, 512], BF16)
        ygT_res = Res()
        mg_res = [Res(), Res()]
        sg_res = [Res(), Res()]
        sbt_res = [Res(), Res()]

        NPS = 8
        ps_t = [es.enter_context(nc.psum_tensor("ps%d" % i, [128, 512], F32)) for i in range(NPS)]
        ps_res = [Res("ps%d" % i, excl=True) for i in range(NPS)]
        ps_free = list(range(NPS))

        def alloc_ps():
            return ps_free.pop(0)

        def rel_ps(i):
            ps_free.append(i)

        def PE(fn, reads, writes):
            return sc.add("pe", fn, reads, writes)

        def ACT(fn, reads, writes):
            return sc.add("act", fn, reads, writes)

        def DVE(fn, reads, writes):
            return sc.add("dve", fn, reads, writes)

        def POOL(fn, reads, writes):
            return sc.add("pool", fn, reads, writes)

        def mm(out, lhsT, rhs, start, stop, reads, writes, tp=None, skip=False):
            kw = {}
            if tp is not None:
                kw["tile_position"] = tp
            if skip:
                kw["skip_group_check"] = True
            PE(lambda e: e.matmul(out, lhsT=lhsT, rhs=rhs, start=start, stop=stop, **kw), reads, writes)

        plan = []
        state = {"added": 0, "done": -1, "next": 0}
        wsc_res = [Res("wsc%d" % i) for i in range(NUNIT)]

        def make_plan():
            def ffn(f, npass):
                for u in range(11):
                    plan.append((U_W13[f] + u, 4096))
                for _ in range(npass):
                    for u in range(6):
                        plan.append((U_W2[f] + u, 4096 if u < 5 else 2048))
            ffn(0, 1)
            for i in (0, 1, 4):
                plan.append((U_WIN + i, 4096))
            ntl = NSEQ * NTILE
            for g in range(ntl):
                ffn(0, 1)
                for i in range(5):
                    plan.append((U_WIN + i, 4096))
                if g > 0:
                    ffn(1, 2)
                for m4 in range(4):
                    plan.append((U_WIN + 5 + m4, 4096))
                    plan.append((U_GLU + m4, 2048))
                plan.append((U_WOUT, 4096))
                plan.append((U_WOUT + 1, 4096))
            ffn(1, 2)

        make_plan()

        def prefetch():
            while state["added"] < len(plan) and state["added"] <= state["done"] + NS:
                r = state["added"]
                unit, ncol = plan[r]
                slot = r % NS
                sc.add("sp", (lambda e, slot=slot, unit=unit, ncol=ncol:
                              e.dma_start(out=wring[:, slot, 0:ncol], in_=wsc_d[unit, :, 0:ncol])),
                       reads=[wsc_res[unit]], writes=[wring_res[slot]], dma="ring%d" % slot)
                state["added"] += 1

        def get(unit):
            r = state["next"]
            assert plan[r][0] == unit, (r, plan[r], unit)
            state["next"] += 1
            prefetch()
            assert state["added"] > r
            return r % NS, r

        def done(r):
            assert r == state["done"] + 1, (r, state["done"])
            state["done"] = r
            prefetch()

        def ld(eng, out, in_, key, writes):
            sc.add(eng, lambda e: e.dma_start(out=out, in_=in_), reads=[], writes=writes, dma=key)

        cr = [const_res]
        ld("pool", ident[:], ident_d, "c0", cr)
        ld("pool", maskp[:], maskp_d, "c0", cr)
        ld("pool", maskc[:], maskc_d, "c0", cr)
        ld("pool", oneh[:], oneh_d, "c0", cr)
        ld("pool", sinkb, sink_d, "c0", cr + hn_res)
        ld("pool", Blb[:], bl_d, "c0", cr)
        ld("sp", gam[:], gam_d, "c1", cr)
        ld("sp", gfin[:], gfin_d, "c1", cr)
        ld("sp", dcol[:], dcol_d, "c1", cr)
        ld("sp", ssmp[:], ssmp_d, "c1", cr)
        ld("sp", cl32, cl_d, "c1", cr + [qT_res])
        DVE(lambda e: e.memset(ones[:], 1.0), [], cr)
        DVE(lambda e: e.memset(eps_t[:], EPS), [], [eps_res])
        DVE(lambda e: e.memset(vmeta[:], 1.0), [], [vmeta_res])
        DVE(lambda e: e.memset(vmeta[:, :, 64:128], 0.0), [], [vmeta_res])
        DVE(lambda e: e.memset(vt[:], 1.0), [], vt_res)
        POOL(lambda e: e.memset(kk[:], 0.0), [], kk_res)
        DVE(lambda e: e.memset(ST[:], 0.0), [], ST_res)

        ssm_p_res = Res("ssmprep")
        spr = [ssm_p_res]

        def T_(i):
            return sp_t[:, i, :]

        ar = ssmp[:, 0, :]
        ai = ssmp[:, 1, :]
        ls = ssmp[:, 2, :]
        ACT(lambda e: e.activation(out=T_(0), in_=ls, func=AF.Exp), cr, spr)
        DVE(lambda e: e.tensor_tensor(out=T_(1), in0=ar, in1=T_(0), op=ALU.mult), cr + spr, spr)
        ACT(lambda e: e.activation(out=Rt[:], in_=T_(1), func=AF.Exp), spr, spr)
        DVE(lambda e: e.tensor_tensor(out=T_(2), in0=ai, in1=T_(0), op=ALU.mult), cr + spr, spr)
        DVE(lambda e: e.tensor_scalar(out=T_(2), in0=T_(2), scalar1=1.0 / (2 * np.pi), scalar2=None,
                                      op0=ALU.mult), spr, spr)
        for thr in (1.0, 1.0, 1.0, 1.0, 0.5):
            DVE(lambda e, thr=thr: e.tensor_scalar(out=T_(3), in0=T_(2), scalar1=thr, scalar2=None,
                                                    op0=ALU.is_ge), spr, spr)
            DVE(lambda e: e.tensor_tensor(out=T_(2), in0=T_(2), in1=T_(3), op=ALU.subtract), spr, spr)
        ACT(lambda e: e.activation(out=T_(4), in_=T_(2), func=AF.Sin, scale=6.28318), spr, spr)
        ACT(lambda e: e.activation(out=T_(5), in_=T_(2), func=AF.Sin, scale=3.14159), spr, spr)
        DVE(lambda e: e.tensor_tensor(out=T_(5), in0=T_(5), in1=T_(5), op=ALU.mult), spr, spr)
        DVE(lambda e: e.tensor_scalar(out=T_(5), in0=T_(5), scalar1=-2.0, scalar2=1.0, op0=ALU.mult,
                                      op1=ALU.add), spr, spr)
        tab_res = Res("tables")
        tr = [tab_res]
        DVE(lambda e: e.memset(COS[:, :, 0:1], 1.0), [], tr)
        DVE(lambda e: e.memset(SIN[:, :, 0:1], 0.0), [], tr)
        DVE(lambda e: e.tensor_copy(out=COS[:, :, 1:2], in_=sp_t[:, 5:6, :].rearrange("p a b -> p b a")), spr, tr)
        DVE(lambda e: e.tensor_copy(out=SIN[:, :, 1:2], in_=sp_t[:, 4:5, :].rearrange("p a b -> p b a")), spr, tr)
        ta = actT[:, 0:16, :].bitcast(F32)
        m = 1
        while m < 128:
            cm = COS[:, :, m:m + 1].to_broadcast([128, 16, m])
            sm = SIN[:, :, m:m + 1].to_broadcast([128, 16, m])
            c_in = COS[:, :, 1:m + 1]
            s_in = SIN[:, :, 1:m + 1]
            t1 = ta[:, :, 0:m]
            t2 = ta[:, :, 64:64 + m]
            DVE(lambda e, c_in=c_in, cm=cm, t1=t1: e.tensor_tensor(out=t1, in0=c_in, in1=cm, op=ALU.mult), tr, [actT_res])
            DVE(lambda e, s_in=s_in, sm=sm, t2=t2: e.tensor_tensor(out=t2, in0=s_in, in1=sm, op=ALU.mult), tr, [actT_res])
            DVE(lambda e, m=m, t1=t1, t2=t2: e.tensor_tensor(out=COS[:, :, m + 1:2 * m + 1], in0=t1, in1=t2,
                                                             op=ALU.subtract), [actT_res, tab_res], tr)
            t3 = ta[:, :, 128:128 + m]
            t4 = ta[:, :, 192:192 + m]
            DVE(lambda e, c_in=c_in, sm=sm, t3=t3: e.tensor_tensor(out=t3, in0=c_in, in1=sm, op=ALU.mult), tr, [actT_res])
            DVE(lambda e, s_in=s_in, cm=cm, t4=t4: e.tensor_tensor(out=t4, in0=s_in, in1=cm, op=ALU.mult), tr, [actT_res])
            DVE(lambda e, m=m, t3=t3, t4=t4: e.tensor_tensor(out=SIN[:, :, m + 1:2 * m + 1], in0=t3, in1=t4,
                                                             op=ALU.add), [actT_res, tab_res], tr)
            m *= 2
        DVE(lambda e: e.tensor_tensor(out=T_(6), in0=Rt[:], in1=T_(5), op=ALU.mult), spr, spr)
        DVE(lambda e: e.tensor_tensor(out=T_(7), in0=Rt[:], in1=T_(4), op=ALU.mult), spr, spr)
        DVE(lambda e: e.tensor_scalar(out=T_(6), in0=T_(6), scalar1=-1.0, scalar2=None, op0=ALU.add), spr, spr)
        DVE(lambda e: e.tensor_tensor(out=T_(8), in0=ar, in1=ar, op=ALU.mult), cr + spr, spr)
        DVE(lambda e: e.tensor_tensor(out=T_(9), in0=ai, in1=ai, op=ALU.mult), cr + spr, spr)
        DVE(lambda e: e.tensor_tensor(out=T_(8), in0=T_(8), in1=T_(9), op=ALU.add), spr, spr)
        DVE(lambda e: e.reciprocal(out=T_(8), in_=T_(8)), spr, spr)
        DVE(lambda e: e.tensor_tensor(out=T_(9), in0=T_(6), in1=ar, op=ALU.mult), cr + spr, spr)
        DVE(lambda e: e.tensor_tensor(out=T_(10), in0=T_(7), in1=ai, op=ALU.mult), cr + spr, spr)
        DVE(lambda e: e.tensor_tensor(out=T_(9), in0=T_(9), in1=T_(10), op=ALU.add), spr, spr)
        DVE(lambda e: e.tensor_tensor(out=T_(9), in0=T_(9), in1=T_(8), op=ALU.mult), spr, spr)
        DVE(lambda e: e.tensor_tensor(out=T_(10), in0=T_(7), in1=ar, op=ALU.mult), cr + spr, spr)
        DVE(lambda e: e.tensor_tensor(out=T_(11), in0=T_(6), in1=ai, op=ALU.mult), cr + spr, spr)
        DVE(lambda e: e.tensor_tensor(out=T_(10), in0=T_(10), in1=T_(11), op=ALU.subtract), spr, spr)
        DVE(lambda e: e.tensor_tensor(out=T_(10), in0=T_(10), in1=T_(8), op=ALU.mult), spr, spr)
        cre = cl32[:, :, 0, :]
        cim = cl32[:, :, 1, :]
        kre = sp_t[:, 9:10, :].rearrange("p a b -> p b a").to_broadcast([128, 16, 32])
        kim = sp_t[:, 10:11, :].rearrange("p a b -> p b a").to_broadcast([128, 16, 32])
        clr = Res("clb")
        crq = cr + spr + [qT_res]
        DVE(lambda e: e.tensor_tensor(out=clt[:, 0], in0=cre, in1=kre, op=ALU.mult), crq, [hnT_res])
        DVE(lambda e: e.tensor_tensor(out=clt[:, 1], in0=cim, in1=kim, op=ALU.mult), crq, [hnT_res])
        DVE(lambda e: e.tensor_tensor(out=clt[:, 2], in0=cre, in1=kim, op=ALU.mult), crq, [hnT_res])
        DVE(lambda e: e.tensor_tensor(out=clt[:, 3], in0=cim, in1=kre, op=ALU.mult), crq, [hnT_res])
        DVE(lambda e: e.tensor_tensor(out=Clb[:, :, 0, :], in0=clt[:, 0], in1=clt[:, 1], op=ALU.subtract), [hnT_res], [clr])
        DVE(lambda e: e.tensor_tensor(out=Clb[:, :, 1, :], in0=clt[:, 1], in1=clt[:, 0], op=ALU.subtract), [hnT_res], [clr])
        DVE(lambda e: e.tensor_tensor(out=clt[:, 2], in0=clt[:, 2], in1=clt[:, 3], op=ALU.add), [hnT_res], [hnT_res])
        DVE(lambda e: e.tensor_scalar(out=Clb[:, :, 2, :], in0=clt[:, 2], scalar1=-1.0, scalar2=None,
                                      op0=ALU.mult), [hnT_res], [clr])
        for kh in range(4):
            pi = alloc_ps()
            mm(ps_t[pi][0:32, :], oneh[0:1, :], sinkb[0:1, kh * 512:(kh + 1) * 512], True, True, cr + hn_res, [ps_res[pi]])
            ACT(lambda e, pi=pi, kh=kh: e.activation(out=Em[:, kh, :], in_=ps_t[pi][0:32, :], func=AF.Exp),
                [ps_res[pi]], [Em_res[kh]])
            rel_ps(pi)

        first_use = []
        seen_u = set()
        for (u, _) in plan:
            if u not in seen_u:
                seen_u.add(u)
                first_use.append(u)
        assert len(first_use) == NUNIT

        def gam_col(u):
            if U_W13[0] <= u < U_W13[0] + 11:
                return 0, "w13"
            if U_W13[1] <= u < U_W13[1] + 11:
                return 16, "w13"
            if U_WIN <= u < U_WIN + 9:
                return 8, "win"
            return None, None

        for n, u in enumerate(first_use):
            b = n % 2
            w32 = [stg32_res[b], wring_res[2 * b], wring_res[2 * b + 1]]
            w16 = [stg16_res[b], actT_res]
            sc.add("sp", lambda e, u=u, b=b: e.dma_start(out=stg32[:, b, :], in_=wts_d[u]), reads=[],
                   writes=w32, dma="stg%d" % b)
            gc, kind = gam_col(u)
            if kind is None:
                for hh in range(2):
                    if hh == 0:
                        DVE(lambda e, b=b, hh=hh: e.tensor_copy(out=stg16[:, b, hh * 2048:(hh + 1) * 2048],
                                                                in_=stg32[:, b, hh * 2048:(hh + 1) * 2048]), w32, w16)
                    else:
                        ACT(lambda e, b=b, hh=hh: e.copy(out=stg16[:, b, hh * 2048:(hh + 1) * 2048],
                                                         in_=stg32[:, b, hh * 2048:(hh + 1) * 2048]), w32, w16)
            else:
                for k in range(8):
                    if kind == "w13":
                        src = stg32[:, b, :].rearrange("p (a k c) -> p a k c", a=4, k=8)[:, :, k, :]
                        dst = stg16[:, b, :].rearrange("p (a k c) -> p a k c", a=4, k=8)[:, :, k, :]
                    else:
                        src = stg32[:, b, k * 512:(k + 1) * 512]
                        dst = stg16[:, b, k * 512:(k + 1) * 512]
                    if k % 2 == 0:
                        DVE(lambda e, src=src, dst=dst, gc=gc, k=k: e.tensor_scalar(
                            out=dst, in0=src, scalar1=gam[:, gc + k:gc + k + 1], scalar2=None, op0=ALU.mult),
                            w32 + cr, w16)
                    else:
                        ACT(lambda e, src=src, dst=dst, gc=gc, k=k: e.mul(out=dst, in_=src, mul=gam[:, gc + k:gc + k + 1]),
                            w32 + cr, w16)
            sc.add("sp", lambda e, u=u, b=b: e.dma_start(out=wsc_d[u], in_=stg16[:, b, :]),
                   reads=w16, writes=[wsc_res[u]], dma="sto%d" % b)

        def norm_T(nblk, ntok, hs, hT, hT_res):
            for b in range(nblk):
                hb = h2[:ntok, hs, b, :]
                hr = h2_res[hs][b]
                sr = stat_res[b]
                ACT(lambda e, hb=hb, b=b: e.activation(out=junk[:ntok, :], in_=hb, func=AF.Square,
                                                       accum_out=stat[:ntok, b:b + 1]),
                    [hr], [junk_res, sr] + sil_res)
                ACT(lambda e, b=b: e.activation(out=stat[:ntok, 4 + b:5 + b], in_=stat[:ntok, b:b + 1], func=AF.Sqrt,
                                                bias=eps_t[:ntok, 0:1], scale=1.0 / D), [sr, eps_res], [sr])
                DVE(lambda e, b=b: e.reciprocal(out=stat[:ntok, 8 + b:9 + b], in_=stat[:ntok, 4 + b:5 + b]), [sr], [sr])
                hb_i = b % 2
                DVE(lambda e, hb=hb, b=b, hb_i=hb_i: e.tensor_scalar(out=hn[:ntok, hb_i, :], in0=hb,
                                                                    scalar1=stat[:ntok, 8 + b:9 + b], scalar2=None,
                                                                    op0=ALU.mult),
                    [hr, sr], [hn_res[hb_i]])
                pi = alloc_ps()
                pv = ps_t[pi][:, :].bitcast(BF16)
                for k in range(8):
                    PE(lambda e, pv=pv, k=k, hb_i=hb_i: e.transpose(
                        out=pv[:, k * ntok:(k + 1) * ntok], in_=hn[:ntok, hb_i, k * 128:(k + 1) * 128],
                        identity=ident[:ntok, :ntok]), [hn_res[hb_i], const_res], [ps_res[pi]])
                for half in range(2):
                    src = pv[:, half * 4 * ntok:(half + 1) * 4 * ntok].rearrange("p (a t) -> p a t", a=4)
                    dst = hT[:, half * 4:(half + 1) * 4, b * ntok:(b + 1) * ntok]
                    if half == 0:
                        ACT(lambda e, src=src, dst=dst: e.copy(out=dst, in_=src), [ps_res[pi]], [hT_res])
                    else:
                        DVE(lambda e, src=src, dst=dst: e.tensor_copy(out=dst, in_=src), [ps_res[pi]], [hT_res])
                rel_ps(pi)
                yield "norm"

        def ffn_gen(f, hs, nblk, ntok, npass_req=2):
            TW = nblk * ntok
            yield from norm_T(nblk, ntok, hs, hnTf, hnTf_res)
            for u in range(11):
                slot, r = get(U_W13[f] + u)
                wr = wring_res[slot]
                for j2 in range(2):
                    j = 2 * u + j2
                    pa = alloc_ps()
                    pb = alloc_ps()
                    for s_, pi in ((0, pa), (1, pb)):
                        for k in range(8):
                            off = ((j2 * 2 + s_) * 8 + k) * 128
                            mm(ps_t[pi][:, :TW], wring[:, slot, off:off + 128], hnTf[:, k, :TW], k == 0, k == 7,
                               [wr, hnTf_res], [ps_res[pi]])
                    si = j % 2
                    ACT(lambda e, pa=pa, si=si: e.activation(out=sil[:, si, :TW], in_=ps_t[pa][:, :TW], func=AF.Silu),
                        [ps_res[pa]], [sil_res[si]])
                    DVE(lambda e, pb=pb, si=si, j=j: e.tensor_tensor(out=actT[:, j, :TW], in0=sil[:, si, :TW],
                                                                    in1=ps_t[pb][:, :TW], op=ALU.mult),
                        [ps_res[pb], sil_res[si]], [actT_res])
                    rel_ps(pa)
                    rel_ps(pb)
                    if j2 == 1:
                        done(r)
                    yield "up"
            npass = npass_req if nblk == 4 else 1
            for pss in range(npass):
                blks = ([2 * pss, 2 * pss + 1] if npass == 2 else [0, 1, 2, 3]) if nblk == 4 else [0]
                banks = {}
                for b in blks:
                    for half in range(2):
                        banks[(b, half)] = alloc_ps()
                for u in range(6):
                    slot, r = get(U_W2[f] + u)
                    for b in blks:
                        for half in range(2):
                            pi = banks[(b, half)]
                            for k4 in range(4):
                                k = 4 * u + k4
                                if k >= FT:
                                    continue
                                off = k4 * 1024 + half * 512
                                mm(ps_t[pi][:ntok, :], actT[:, k, b * ntok:(b + 1) * ntok], wring[:, slot, off:off + 512],
                                   k == 0, k == FT - 1, [wring_res[slot], actT_res], [ps_res[pi]])
                    done(r)
                    yield "down"
                for (b, half), pi in banks.items():
                    DVE(lambda e, pi=pi, b=b, half=half: e.scalar_tensor_tensor(
                        out=h2[:ntok, hs, b, half * 512:(half + 1) * 512], in0=ps_t[pi][:ntok, :], scalar=0.5,
                        in1=h2[:ntok, hs, b, half * 512:(half + 1) * 512], op0=ALU.mult, op1=ALU.add),
                        [ps_res[pi], h2_res[hs][b]], [h2_res[hs][b]])
                    rel_ps(pi)
                yield "down"

        def final_gen(hs, s, t):
            for b in range(4):
                sr = stat_res[b]
                hr = h2_res[hs][b]
                ACT(lambda e, b=b: e.activation(out=junk[:, :], in_=h2[:, hs, b, :], func=AF.Square,
                                                accum_out=stat[:, b:b + 1]), [hr], [junk_res, sr] + sil_res)
                ACT(lambda e, b=b: e.activation(out=stat[:, 4 + b:5 + b], in_=stat[:, b:b + 1], func=AF.Sqrt,
                                                bias=eps_t[:, 0:1], scale=1.0 / D), [sr, eps_res], [sr])
                DVE(lambda e, b=b: e.reciprocal(out=stat[:, 8 + b:9 + b], in_=stat[:, 4 + b:5 + b]), [sr], [sr])
                oi = b % 2
                DVE(lambda e, b=b, oi=oi: e.scalar_tensor_tensor(
                    out=ostage[:, oi, :], in0=h2[:, hs, b, :], scalar=stat[:, 8 + b:9 + b], in1=gfin[:], op0=ALU.mult,
                    op1=ALU.mult), [hr, sr, const_res], [ost_res[oi], qT_res])
                r0 = t * 512 + b * 128
                sc.add("pool", lambda e, s=s, r0=r0, oi=oi: e.dma_start(out=y_d[s, r0:r0 + 128, :], in_=ostage[:, oi, :]),
                       reads=[ost_res[oi], qT_res], writes=[Res()], dma="yo%d" % oi)
                yield "final"

        def run_all(gen):
            for _ in gen:
                pass

        def proj_fm(slot, coff, ncols_tile, TW, evac):
            pi = alloc_ps()
            for k in range(8):
                mm(ps_t[pi][:, :TW], wring[:, slot, k * 512 + coff:k * 512 + coff + 128], hnT[:, k, :TW], k == 0, k == 7,
                   [wring_res[slot], hnT_res], [ps_res[pi]])
            evac(pi)
            rel_ps(pi)

        def ssm_chunk(L, cols, meta, seq_first, yfi, inject=None):
            if seq_first and not meta:
                DVE(lambda e: e.tensor_copy(out=ST[:], in_=ST0[:]), [ST0_res], ST_res)
            py = None
            if not meta:
                py = alloc_ps()

            def S1(p):
                ct, q = p // 4, p % 4
                bi = p % 2
                px = alloc_ps()
                for part in range(2):
                    mm(ps_t[px][:, part * 128:part * 128 + L], Blb[32 * q:32 * q + 32, ct, part, :],
                       uTb[32 * q:32 * q + 32, ct, cols], True, True, [const_res, uT_res], [ps_res[px]], tp=(32 * q, 0))
                pxv = ps_t[px][:, 0:256].rearrange("p (a t) -> p a t", a=2)[:, :, 0:L]
                cosb = COS[:, p:p + 1, 0:L].to_broadcast([128, 2, L])
                sinb = SIN[:, p:p + 1, 0:L].to_broadcast([128, 2, L])
                DVE(lambda e: e.tensor_tensor(out=TC[:, bi, :, 0:L], in0=pxv, in1=cosb, op=ALU.mult),
                    [ps_res[px], tab_res], [ssm_res["TC"][bi]])
                DVE(lambda e: e.tensor_tensor(out=TS[:, bi, :, 0:L], in0=pxv, in1=sinb, op=ALU.mult),
                    [ps_res[px], tab_res], [ssm_res["TS"][bi]])
                rel_ps(px)

            def S2(p):
                bi = p % 2
                POOL(lambda e: e.tensor_tensor(out=Wt[:, bi, 0, 0:L], in0=TC[:, bi, 0, 0:L], in1=TS[:, bi, 1, 0:L],
                                               op=ALU.add),
                     [ssm_res["TC"][bi], ssm_res["TS"][bi]], [ssm_res["W"][bi]])
                POOL(lambda e: e.tensor_tensor(out=Wt[:, bi, 1, 0:L], in0=TC[:, bi, 1, 0:L], in1=TS[:, bi, 0, 0:L],
                                               op=ALU.subtract),
                     [ssm_res["TC"][bi], ssm_res["TS"][bi]], [ssm_res["W"][bi]])

            def S3(p):
                bi = p % 2
                for part in range(2):
                    DVE(lambda e, part=part: e.tensor_tensor_scan(
                        out=Zt[:, bi, part, 0:L], data0=Rt[:, p:p + 1].to_broadcast([128, L]),
                        data1=Wt[:, bi, part, 0:L], initial=ST[:, p, part:part + 1], op0=ALU.mult, op1=ALU.add),
                        [ssm_res["W"][bi], ST_res[p], ssm_p_res], [ssm_res["Z"][bi]])
                DVE(lambda e: e.tensor_copy(out=zl[:, p, :], in_=Zt[:, bi, :, L - 1]), [ssm_res["Z"][bi]], [zl_res])

            def S4(p):
                bi = p % 2
                cosb2 = COS[:, p:p + 1, 0:L].to_broadcast([128, 2, L])
                sinb2 = SIN[:, p:p + 1, 0:L].to_broadcast([128, 2, L])
                POOL(lambda e: e.tensor_tensor(out=QC[:, bi, :, 0:L], in0=Zt[:, bi, :, 0:L], in1=cosb2, op=ALU.mult),
                     [ssm_res["Z"][bi], tab_res], [ssm_res["QC"][bi]])
                POOL(lambda e: e.tensor_tensor(out=QS[:, bi, :, 0:L], in0=Zt[:, bi, :, 0:L], in1=sinb2, op=ALU.mult),
                     [ssm_res["Z"][bi], tab_res], [ssm_res["QS"][bi]])

            def S5(p):
                ct, q = p // 4, p % 4
                bi = p % 2
                yo = ps_t[py][32 * q:32 * q + 32, ct * 128:ct * 128 + L]
                rq = [ssm_res["QC"][bi], ssm_res["QS"][bi], clr]
                mm(yo, Clb[:, p, 0, :], QC[:, bi, 0, 0:L], True, False, rq, [ps_res[py]], tp=(0, 32 * q))
                mm(yo, Clb[:, p, 1, :], QS[:, bi, 1, 0:L], False, False, rq, [ps_res[py]], tp=(0, 32 * q))
                mm(yo, Clb[:, p, 2, :], QS[:, bi, 0, 0:L], False, False, rq, [ps_res[py]], tp=(0, 32 * q))
                mm(yo, Clb[:, p, 2, :], QC[:, bi, 1, 0:L], False, True, rq, [ps_res[py]], tp=(0, 32 * q))

            for i in range(19):
                if i < 16:
                    S1(i)
                if 1 <= i <= 16:
                    S2(i - 1)
                if 2 <= i <= 17 and not meta:
                    S4(i - 2)
                if 1 <= i <= 16:
                    S3(i - 1)
                if 3 <= i <= 18 and not meta:
                    S5(i - 3)
                if inject is not None:
                    inject(i)
            cLv = COS[:, :, L]
            sLv = SIN[:, :, L]
            zr = zl[:, :, 0]
            zi = zl[:, :, 1]
            rr = [zl_res, tab_res]
            DVE(lambda e: e.tensor_tensor(out=ct4[:, 0, :], in0=zr, in1=cLv, op=ALU.mult), rr, [ct_res])
            DVE(lambda e: e.tensor_tensor(out=ct4[:, 1, :], in0=zi, in1=sLv, op=ALU.mult), rr, [ct_res])
            DVE(lambda e: e.tensor_tensor(out=ct4[:, 2, :], in0=zr, in1=sLv, op=ALU.mult), rr, [ct_res])
            DVE(lambda e: e.tensor_tensor(out=ct4[:, 3, :], in0=zi, in1=cLv, op=ALU.mult), rr, [ct_res])
            DVE(lambda e: e.tensor_tensor(out=ST[:, :, 0], in0=ct4[:, 0, :], in1=ct4[:, 1, :], op=ALU.subtract),
                [ct_res], ST_res)
            DVE(lambda e: e.tensor_tensor(out=ST[:, :, 1], in0=ct4[:, 2, :], in1=ct4[:, 3, :], op=ALU.add),
                [ct_res], ST_res)
            if meta:
                return
            for ct in range(4):
                DVE(lambda e, ct=ct: e.scalar_tensor_tensor(
                    out=yf[:, yfi, ct * 128:(ct + 1) * 128], in0=uTb[:, ct, cols], scalar=dcol[:, ct:ct + 1],
                    in1=ps_t[py][:, ct * 128:(ct + 1) * 128], op0=ALU.mult, op1=ALU.add),
                    [ps_res[py], uT_res, const_res], [yf_res[yfi]])
            rel_ps(py)
            g0 = gl[:, yfi, 0, :]
            g1 = gl[:, yfi, 1, :]
            yv = yf[:, yfi, :]
            POOL(lambda e: e.tensor_tensor(out=g0, in0=yv, in1=yv, op=ALU.mult), [yf_res[yfi], actT_res], [gl_res[yfi]])
            POOL(lambda e: e.tensor_scalar(out=g0, in0=g0, scalar1=0.044715, scalar2=1.0, op0=ALU.mult, op1=ALU.add),
                 [gl_res[yfi], actT_res], [gl_res[yfi]])
            POOL(lambda e: e.tensor_tensor(out=g0, in0=g0, in1=yv, op=ALU.mult), [gl_res[yfi], yf_res[yfi], actT_res],
                 [gl_res[yfi]])
            ACT(lambda e: e.activation(out=g1, in_=g0, func=AF.Sigmoid, scale=GELU_C), [gl_res[yfi], actT_res],
                [gl_res[yfi]])
            dst = ygT[:, :, cols]
            DVE(lambda e: e.tensor_tensor(out=dst, in0=yv.rearrange("p (a t) -> p a t", a=4), in1=g1.rearrange(
                "p (a t) -> p a t", a=4), op=ALU.mult), [gl_res[yfi], yf_res[yfi], actT_res], [ygT_res])

        def dump(name, src_ap, reads, idx=None):
            if name not in dbg_d:
                return
            dst = dbg_d[name] if idx is None else dbg_d[name][idx]
            n = state.setdefault("ndbg", 0)
            state["ndbg"] = n + 1
            sc.add("pool", lambda e: e.dma_start(out=dst, in_=src_ap), reads=reads, writes=[Res()], dma="dbg%d" % n)

        sc.add("pool", lambda e: e.dma_start(out=h2[0:16, 0, 0, :], in_=meta_d), reads=[], writes=[h2_res[0][0]], dma="xl0")
        run_all(ffn_gen(0, 0, 1, 16))
        run_all(norm_T(1, 16, 0, hnT, hnT_res))
        slot, r = get(U_WIN + 0)
        for kh in range(4):
            def ev(pi, kh=kh):
                ACT(lambda e: e.copy(out=kk[0:64, kh, 0, 0:16], in_=ps_t[pi][0:64, 0:16]), [ps_res[pi]], [kk_res[0]])
                ACT(lambda e: e.copy(out=kk[64:128, kh, 1, 0:16], in_=ps_t[pi][64:128, 0:16]), [ps_res[pi]], [kk_res[0]])
            proj_fm(slot, kh * 128, 128, 16, ev)
        done(r)
        slot, r = get(U_WIN + 1)
        pi = alloc_ps()
        for k in range(8):
            mm(ps_t[pi][0:16, 0:256], hnT[:, k, 0:16], wring[:, slot, k * 512:k * 512 + 256], k == 0, k == 7,
               [wring_res[slot], hnT_res], [ps_res[pi]])
        ACT(lambda e, pi=pi: e.copy(out=vmeta[0:16, :, 64:128], in_=ps_t[pi][0:16, 0:256].rearrange("p (a d) -> p a d", a=4)),
            [ps_res[pi]], [vmeta_res])
        rel_ps(pi)
        done(r)
        slot, r = get(U_WIN + 4)
        for ct in range(4):
            def ev(pi, ct=ct):
                DVE(lambda e: e.tensor_copy(out=uTb[:, ct, 0:16], in_=ps_t[pi][:, 0:16]), [ps_res[pi]], [uT_res])
            proj_fm(slot, ct * 128, 128, 16, ev)
        done(r)
        ssm_chunk(16, slice(0, 16), True, True, 0)
        DVE(lambda e: e.tensor_copy(out=ST0[:], in_=ST[:]), ST_res, [ST0_res])

        tiles = [(s, t) for s in range(NSEQ) for t in range(NTILE)]

        def load_x(g):
            s, t = tiles[g]
            hs = g % 2
            for b in range(4):
                r0 = t * 512 + b * 128
                sc.add("pool", lambda e, s=s, r0=r0, b=b, hs=hs: e.dma_start(out=h2[:, hs, b, :], in_=x_d[s, r0:r0 + 128, :]),
                       reads=[], writes=[h2_res[hs][b]], dma="xl%d" % b)

        def SA(b, kh):
            gb = cur['t'] * 4 + b
            qc = slice(b * 128, (b + 1) * 128)
            emi = kh
            p0 = alloc_ps()
            keyt = [("m", p0, None)]
            if gb > 0:
                keyt.append(("p", alloc_ps(), maskp))
            keyt.append(("c", alloc_ps(), maskc))
            ebufs = []
            for kind, pi, msk in keyt:
                if kind == "m":
                    kcols = slice(0, 16)
                    nk = 16
                    kres = kk_res[0]
                else:
                    sl = b if kind == "p" else b + 1
                    kcols = slice(16 + sl * 128, 16 + (sl + 1) * 128)
                    nk = 128
                    kres = kk_res[1 + sl]
                for half in range(2):
                    o = ps_t[pi][0:nk, half * 256:(half + 1) * 256].rearrange("p (a t) -> p a t", a=2)
                    mm(o, kk[:, kh, half, kcols], qT[:, 2 * kh:2 * kh + 2, qc], half == 0, msk is None,
                       [kres, qT_res], [ps_res[pi]], skip=True)
                if msk is not None:
                    mm(ps_t[pi][:, :], ident[:], msk[:], False, True, [const_res], [ps_res[pi]], skip=True)
                if kind == "m":
                    ACT(lambda e, pi=pi, emi=emi: e.activation(out=Em[0:16, emi, :], in_=ps_t[pi][0:16, :],
                                                              func=AF.Exp, scale=0.125),
                        [ps_res[pi]], [Em_res[emi]])
                    ebufs.append((Em[0:17, emi, :], Em_res[emi], 17, vmeta[0:17, kh, :], vmeta_res))
                else:
                    ei = state.setdefault("ei", 0)
                    state["ei"] = (ei + 1) % 4
                    ACT(lambda e, pi=pi, ei=ei: e.activation(out=Eb[:, ei, :], in_=ps_t[pi][:, :],
                                                            func=AF.Exp, scale=0.125),
                        [ps_res[pi]], [Eb_res[ei]])
                    vs = b if kind == "p" else b + 1
                    ebufs.append((Eb[:, ei, :], Eb_res[ei], 128, vt[:, vs, kh, :], vt_res[vs]))
                rel_ps(pi)
            return ebufs

        def SB(b, kh, ebufs):
            qc = slice(b * 128, (b + 1) * 128)
            pn = alloc_ps()
            for idx, (E, Er, K, V, Vr) in enumerate(ebufs):
                sp_ = idx == len(ebufs) - 1
                mm(ps_t[pn][:, 0:256], V[:, 64:192], E[:, 0:256], idx == 0, sp_, [Er, Vr], [ps_res[pn]], skip=True)
                mm(ps_t[pn][:, 256:512], V[:, 0:128], E[:, 256:512], False, sp_, [Er, Vr], [ps_res[pn]], skip=True)
            ri = kh % 2
            ACT(lambda e: e.activation(out=rden[0:64, ri, :], in_=ps_t[pn][64:128, 0:256], func=AF.Ln),
                [ps_res[pn]], [rden_res[ri]])
            ACT(lambda e: e.activation(out=rden[64:128, ri, :], in_=ps_t[pn][0:64, 256:512], func=AF.Ln),
                [ps_res[pn]], [rden_res[ri]])
            ACT(lambda e: e.activation(out=rden[:, ri, :], in_=rden[:, ri, :], func=AF.Exp, scale=-1.0),
                [rden_res[ri]], [rden_res[ri]])
            for half in range(2):
                rows = slice(half * 64, (half + 1) * 64)
                DVE(lambda e, rows=rows, half=half: e.tensor_tensor(
                    out=attnT[rows, 2 * kh:2 * kh + 2, qc],
                    in0=ps_t[pn][rows, half * 256:(half + 1) * 256].rearrange("p (a t) -> p a t", a=2),
                    in1=rden[rows, ri, :].rearrange("p (a t) -> p a t", a=2), op=ALU.mult),
                    [ps_res[pn], rden_res[ri], actT_res], [attnT_res[2 * kh], attnT_res[2 * kh + 1]])
            rel_ps(pn)


        cur = {"t": 0}
        load_x(0)
        prevF = None
        for g, (s, t) in enumerate(tiles):
            first = (g == 0)
            hs = g % 2
            cur["t"] = t
            run_all(ffn_gen(0, hs, 4, 128, npass_req=1))
            if first:
                for b in range(4):
                    dump("h1", h2[:, hs, b, :], [h2_res[hs][b]], idx=b)
            run_all(norm_T(4, 128, hs, hnT, hnT_res))
            slot, r = get(U_WIN + 0)
            for kh in range(4):
                def ev(pi, kh=kh):
                    for hv in range(2):
                        rows = slice(hv * 64, (hv + 1) * 64)
                        if kh % 2 == 0:
                            ACT(lambda e, rows=rows, hv=hv: e.copy(out=kk[rows, kh, hv, 144:656], in_=ps_t[pi][rows, :]),
                                [ps_res[pi]], kk_res[2:6])
                        else:
                            DVE(lambda e, rows=rows, hv=hv: e.tensor_copy(out=kk[rows, kh, hv, 144:656],
                                                                         in_=ps_t[pi][rows, :]),
                                [ps_res[pi]], kk_res[2:6])
                proj_fm(slot, kh * 128, 128, 512, ev)
            done(r)
            slot, r = get(U_WIN + 1)
            for b in range(4):
                pi = alloc_ps()
                for k in range(8):
                    mm(ps_t[pi][:, 0:256], hnT[:, k, b * 128:(b + 1) * 128], wring[:, slot, k * 512:k * 512 + 256],
                       k == 0, k == 7, [wring_res[slot], hnT_res], [ps_res[pi]])
                ACT(lambda e, pi=pi, b=b: e.copy(out=vt[:, 1 + b, :, 64:128],
                                                 in_=ps_t[pi][:, 0:256].rearrange("p (a d) -> p a d", a=4)), [ps_res[pi]],
                    [vt_res[1 + b]])
                rel_ps(pi)
            done(r)
            for qu in range(2):
                slot, r = get(U_WIN + 2 + qu)
                for jj in range(4):
                    j = qu * 4 + jj

                    def ev(pi, j=j):
                        if j % 2 == 0:
                            ACT(lambda e: e.copy(out=qT[:, j, :], in_=ps_t[pi][:, :]), [ps_res[pi]], [qT_res])
                        else:
                            DVE(lambda e: e.tensor_copy(out=qT[:, j, :], in_=ps_t[pi][:, :]), [ps_res[pi]], [qT_res])
                    proj_fm(slot, jj * 128, 128, 512, ev)
                done(r)
            slot, r = get(U_WIN + 4)
            for ct in range(4):
                def ev(pi, ct=ct):
                    if ct % 2 == 0:
                        ACT(lambda e: e.copy(out=uTb[:, ct, :], in_=ps_t[pi][:, :]), [ps_res[pi]], [uT_res])
                    else:
                        DVE(lambda e: e.tensor_copy(out=uTb[:, ct, :], in_=ps_t[pi][:, :]), [ps_res[pi]], [uT_res])
                proj_fm(slot, ct * 128, 128, 512, ev)
            done(r)
            items = [(b, kh) for b in range(4) for kh in range(4)]
            att = {"k": 0, "pend": None}

            def att_step():
                k = att["k"]
                if k < 16:
                    b_, kh_ = items[k]
                    eb = SA(b_, kh_)
                    if att["pend"] is not None:
                        SB(*att["pend"])
                    att["pend"] = (b_, kh_, eb)
                elif k == 16:
                    SB(*att["pend"])
                    att["pend"] = None
                att["k"] = k + 1

            fst = {"gen": prevF, "n": 0, "live": prevF is not None}

            def f_step(allow_down):
                if not fst["live"]:
                    return
                if fst["n"] >= 26 and not allow_down:
                    return
                try:
                    next(fst["gen"])
                    fst["n"] += 1
                except StopIteration:
                    fst["live"] = False

            slot_n = {"n": 0}

            def inj(i):
                n = slot_n["n"]
                slot_n["n"] = n + 1
                if att["k"] <= 16:
                    att_step()
                if ((n + 1) * 45) // 75 > (n * 45) // 75:
                    f_step(att["k"] > 16)

            for c in range(4):
                ssm_chunk(128, slice(c * 128, (c + 1) * 128), False, (t == 0 and c == 0), 0, inject=inj)
            while att["k"] <= 16:
                att_step()
            while fst["live"]:
                f_step(True)
            if first:
                for j in range(8):
                    dump("attn", attnT[:, j, :], [attnT_res[j]], idx=j)
                dump("yg", ygT[:, :, :], [ygT_res])
            if g + 1 < len(tiles):
                load_x(g + 1)
            for m4 in range(4):
                slot, r = get(U_WIN + 5 + m4)
                slotG, rG = get(U_GLU + m4)
                for jj in range(2):
                    j = 2 * m4 + jj
                    bi = j % 2
                    pga = alloc_ps()
                    pgs = alloc_ps()
                    for k in range(8):
                        mm(ps_t[pga][:, :], wring[:, slot, k * 512 + jj * 128:k * 512 + jj * 128 + 128], hnT[:, k, :],
                           k == 0, k == 7, [wring_res[slot], hnT_res], [ps_res[pga]])
                    for k in range(8):
                        mm(ps_t[pgs][:, :], wring[:, slot, k * 512 + 256 + jj * 128:k * 512 + 256 + jj * 128 + 128],
                           hnT[:, k, :], k == 0, k == 7, [wring_res[slot], hnT_res], [ps_res[pgs]])
                    pA = alloc_ps()
                    pB = alloc_ps()
                    for ct in range(4):
                        mm(ps_t[pA][:, :], wring[:, slotG, ct * 512 + jj * 128:ct * 512 + jj * 128 + 128], ygT[:, ct, :],
                           ct == 0, ct == 3, [wring_res[slotG], ygT_res], [ps_res[pA]])
                    for ct in range(4):
                        mm(ps_t[pB][:, :], wring[:, slotG, ct * 512 + 256 + jj * 128:ct * 512 + 256 + jj * 128 + 128],
                           ygT[:, ct, :], ct == 0, ct == 3, [wring_res[slotG], ygT_res], [ps_res[pB]])
                    ACT(lambda e, pga=pga, bi=bi: e.activation(out=sg[:, bi, 0, :], in_=ps_t[pga][:, :],
                                                              func=AF.Sigmoid), [ps_res[pga], actT_res], [sg_res[bi]])
                    ACT(lambda e, pgs=pgs, bi=bi: e.activation(out=sg[:, bi, 1, :], in_=ps_t[pgs][:, :],
                                                              func=AF.Sigmoid), [ps_res[pgs], actT_res], [sg_res[bi]])
                    ACT(lambda e, pB=pB, bi=bi: e.activation(out=sbt[:, bi, :], in_=ps_t[pB][:, :], func=AF.Sigmoid),
                        [ps_res[pB]], [sbt_res[bi]] + ssm_res["TC"] + ssm_res["TS"])
                    rel_ps(pga)
                    rel_ps(pgs)
                    rel_ps(pB)
                    DVE(lambda e, pA=pA, bi=bi: e.tensor_tensor(out=mg[:, bi, 0, :], in0=ps_t[pA][:, :], in1=sbt[:, bi, :],
                                                               op=ALU.mult),
                        [ps_res[pA], sbt_res[bi], actT_res] + ssm_res["TC"] + ssm_res["TS"], [mg_res[bi]])
                    rel_ps(pA)
                    POOL(lambda e, bi=bi: e.tensor_tensor(out=mg[:, bi, 0, :], in0=mg[:, bi, 0, :], in1=sg[:, bi, 1, :],
                                                          op=ALU.mult), [mg_res[bi], sg_res[bi], actT_res], [mg_res[bi]])
                    POOL(lambda e, bi=bi, j=j: e.tensor_tensor(out=mg[:, bi, 1, :], in0=attnT[:, j, :], in1=sg[:, bi, 0, :],
                                                               op=ALU.mult),
                         [attnT_res[j], sg_res[bi], mg_res[bi], actT_res], [mg_res[bi]])
                    DVE(lambda e, bi=bi, j=j: e.tensor_tensor(out=attnT[:, j, :], in0=mg[:, bi, 0, :], in1=mg[:, bi, 1, :],
                                                              op=ALU.add), [mg_res[bi], actT_res], [attnT_res[j]])
                done(r)
                done(rG)
            if first:
                for j in range(8):
                    dump("merged", attnT[:, j, :], [attnT_res[j]], idx=j)
            so = [get(U_WOUT), get(U_WOUT + 1)]
            for b in range(4):
                for half in range(2):
                    pi = alloc_ps()
                    for k in range(8):
                        slot, r = so[k // 4]
                        off = (k % 4) * 1024 + half * 512
                        mm(ps_t[pi][:, :], attnT[:, k, b * 128:(b + 1) * 128], wring[:, slot, off:off + 512], k == 0,
                           k == 7, [wring_res[slot], attnT_res[k]], [ps_res[pi]])
                    DVE(lambda e, pi=pi, b=b, half=half, hs=hs: e.tensor_tensor(
                        out=h2[:, hs, b, half * 512:(half + 1) * 512], in0=ps_t[pi][:, :],
                        in1=h2[:, hs, b, half * 512:(half + 1) * 512], op=ALU.add),
                        [ps_res[pi], h2_res[hs][b]], [h2_res[hs][b]])
                    rel_ps(pi)
            for slot, r in so:
                done(r)
            POOL(lambda e: e.tensor_copy(out=kk[:, :, :, 16:144], in_=kk[:, :, :, 16 + 4 * 128:16 + 5 * 128]),
                 [kk_res[5]], [kk_res[1]])
            POOL(lambda e: e.tensor_copy(out=vt[:, 0, :, :], in_=vt[:, 4, :, :]), [vt_res[4]], [vt_res[0]])
            if first:
                for b in range(4):
                    dump("h2", h2[:, hs, b, :], [h2_res[hs][b]], idx=b)

            def chain(hs=hs, s=s, t=t):
                yield from ffn_gen(1, hs, 4, 128)
                yield from final_gen(hs, s, t)
            prevF = chain()
        run_all(prevF)

        assert state["next"] == len(plan), (state["next"], len(plan))
        sc.finalize()
        sem_keys = [("E", e) for e in Sched.ENGS] + sorted(sc.dma_counts.keys())
        sems = {}
        for i, k in enumerate(sem_keys):
            sems[k] = es.enter_context(nc.semaphore("s%d" % i))
        block = es.enter_context(nc.Block())
        out_keys = sorted(sc.dma_counts.keys())

        @block.tensor
        def _(e):
            sc.emit("pe", e, sems)

        @block.scalar
        def _(e):
            sc.emit("act", e, sems)

        @block.vector
        def _(e):
            sc.emit("dve", e, sems)

        @block.gpsimd
        def _(e):
            sc.emit("pool", e, sems)
            for k in out_keys:
                e.wait_ge(sems[k], sc.dma_counts[k])

        @block.sync
        def _(e):
            sc.emit("sp", e, sems)
    return nc, sc


def _prep_shared(inp):
    f32 = np.float32
    g = lambda k: np.asarray(inp[k], dtype=f32)

    def w13_units(w1, w3):
        W = np.stack([w1, w3], 0).reshape(2, 8, 128, 11, 2, 128)
        return np.ascontiguousarray(W.transpose(3, 2, 4, 0, 1, 5)).reshape(11, 128, 4096)

    def w2_units(w2):
        W = np.zeros((24, 128, 1024), f32)
        W[:22] = w2.reshape(22, 128, 1024)
        return np.ascontiguousarray(W.reshape(6, 4, 128, 1024).transpose(0, 2, 1, 3)).reshape(6, 128, 4096)

    def win_units(w_in):
        q = w_in[:, 0:1024]
        k = w_in[:, 1024:1280]
        v = w_in[:, 1280:1536]
        u = w_in[:, 1536:2048]
        ga = w_in[:, 2048:3072]
        gs = w_in[:, 3072:4096]
        kdup = np.concatenate([k[:, (kh // 2) * 64:(kh // 2 + 1) * 64] for kh in range(8)], 1)
        vpad = np.concatenate([v, np.zeros((1024, 256), f32)], 1)
        cols = [kdup, vpad, q[:, :512], q[:, 512:], u]
        for m in range(4):
            cols.append(np.concatenate([ga[:, m * 256:(m + 1) * 256], gs[:, m * 256:(m + 1) * 256]], 1))
        return np.stack([np.ascontiguousarray(c.reshape(8, 128, 512).transpose(1, 0, 2)).reshape(128, 4096) for c in cols])

    wts = np.zeros((NUNIT, 128, 4096), f32)
    wts[0:11] = w13_units(g("ffn1_w1")[0], g("ffn1_w3")[0])
    wts[11:17] = w2_units(g("ffn1_w2")[0])
    wts[17:26] = win_units(g("w_in")[0])
    ga_ = g("ssm_glu_a")[0].reshape(4, 128, 1024)
    gb_ = g("ssm_glu_b")[0].reshape(4, 128, 1024)
    for m4 in range(4):
        blk = np.concatenate([ga_[:, :, m4 * 256:(m4 + 1) * 256], gb_[:, :, m4 * 256:(m4 + 1) * 256]], axis=2)
        wts[26 + m4, :, 0:2048] = np.ascontiguousarray(blk.transpose(1, 0, 2)).reshape(128, 2048)
    wts[30:32] = np.ascontiguousarray(g("w_out")[0].reshape(2, 4, 128, 1024).transpose(0, 2, 1, 3)).reshape(2, 128, 4096)
    wts[32:43] = w13_units(g("ffn2_w1")[0], g("ffn2_w3")[0])
    wts[43:49] = w2_units(g("ffn2_w2")[0])

    gam = np.zeros((128, 24), f32)
    for n, key in enumerate(("ffn1_norm", "mix_norm", "ffn2_norm")):
        gam[:, n * 8:(n + 1) * 8] = g(key)[0].reshape(8, 128).T
    gfin = np.ascontiguousarray(np.broadcast_to(g("final_norm")[None, :], (128, D)))

    a_re = g("ssm_a_re")[0]
    a_im = g("ssm_a_im")[0]
    lstep = g("ssm_log_step")[0]
    ssmp = np.zeros((128, 3, 16), f32)
    for p in range(16):
        for g2 in range(2):
            gi = 2 * p + g2
            ssmp[g2 * 64:(g2 + 1) * 64, 0, p] = a_re[gi]
            ssmp[g2 * 64:(g2 + 1) * 64, 1, p] = a_im[gi]
            ssmp[g2 * 64:(g2 + 1) * 64, 2, p] = lstep[gi]
    b_re = g("ssm_b_re")[0]
    b_im = g("ssm_b_im")[0]
    c_re = g("ssm_c_re")[0]
    c_im = g("ssm_c_im")[0]
    bl = np.zeros((128, 4, 2, 128), f32)
    cl = np.zeros((128, 16, 2, 32), f32)
    for p in range(16):
        ct, q = p // 4, p % 4
        for g2 in range(2):
            gi = 2 * p + g2
            rows = slice(32 * q + 16 * g2, 32 * q + 16 * g2 + 16)
            bl[rows, ct, 0, g2 * 64:(g2 + 1) * 64] = b_re[gi].T
            bl[rows, ct, 1, g2 * 64:(g2 + 1) * 64] = b_im[gi].T
            cl[g2 * 64:(g2 + 1) * 64, p, 0, g2 * 16:(g2 + 1) * 16] = c_re[gi].T
            cl[g2 * 64:(g2 + 1) * 64, p, 1, g2 * 16:(g2 + 1) * 16] = c_im[gi].T
    dcol = np.ascontiguousarray(g("ssm_d")[0].reshape(4, 128).T)
    sinks = g("attn_sinks")[0]
    sinkrow = np.zeros((1, 2048), f32)
    for kh in range(4):
        for half in range(2):
            for i in range(2):
                o = kh * 512 + half * 256 + i * 128
                sinkrow[0, o:o + 128] = sinks[4 * kh + 2 * i + half]
    kj = np.arange(128)[:, None]
    qi = np.arange(128)[None, :]
    maskp = np.where(kj > qi, 0.0, NEG).astype(f32)
    maskc = np.where(kj <= qi, 0.0, NEG).astype(f32)
    onehot = np.zeros((1, 32), f32)
    onehot[0, 16] = 1.0
    return {
        "meta": g("meta_tokens"), "wts": wts, "gam": gam, "gfin": gfin, "ssmp": ssmp, "bl": bl, "cl": cl,
        "dcol": dcol, "sinkrow": sinkrow, "ident": np.eye(128, dtype=f32),
        "maskp": np.tile(maskp, (1, 4)), "maskc": np.tile(maskc, (1, 4)), "onehot": onehot,
    }


_CACHE = {}


def kernel(**inputs):
    x = np.asarray(inputs["x"], dtype=np.float32)
    B, S, _ = x.shape
    ncores = 8
    nseq = B // ncores
    ntile = S // 512
    key = (nseq, ntile)
    if key not in _CACHE:
        _CACHE[key] = build(nseq, ntile)[0]
    nc = _CACHE[key]
    shared = _prep_shared(inputs)
    in_maps = []
    for c in range(ncores):
        m = dict(shared)
        m["x"] = np.ascontiguousarray(x[c * nseq:(c + 1) * nseq])
        in_maps.append(m)
    res = run_bass_kernel_spmd(nc, in_maps, core_ids=list(range(ncores)))
    return np.concatenate([r["y"] for r in res.results], axis=0)
```

```python
import numpy as np
from contextlib import ExitStack
import concourse.bass as bass
import concourse.mybir as mybir
from concourse.bass_utils import run_bass_kernel_spmd

F32 = mybir.dt.float32
BF16 = mybir.dt.bfloat16
AF = mybir.ActivationFunctionType
ALU = mybir.AluOpType

D = 1024
FF = 2816
FT = 22
NS = 4
NEG = -1.0e6
GELU_C = 1.5957691216057308
EPS = 1e-6

U_W13 = (0, 32)
U_W2 = (11, 43)
U_WIN = 17
U_GLU = 26
U_WOUT = 30
NUNIT = 49


class Res:
    __slots__ = ("w", "rs", "name", "excl")

    def __init__(self, name="", excl=False):
        self.w = None
        self.rs = []
        self.name = name
        self.excl = excl


class Op:
    __slots__ = ("eng", "fn", "deps", "needed", "semkey", "count")


class Sched:
    ENGS = ("pe", "act", "dve", "pool", "sp")

    def __init__(self):
        self.q = {e: [] for e in self.ENGS}
        self.dma_counts = {}

    def add(self, eng, fn, reads=(), writes=(), dma=None, ndma=1):
        op = Op()
        op.eng = eng
        op.fn = fn
        op.deps = []
        op.needed = False
        op.semkey = dma
        op.count = 0
        if dma is not None:
            c = self.dma_counts.get(dma, 0) + 16 * ndma
            self.dma_counts[dma] = c
            op.count = c
            op.needed = True
        seen = set()

        def dep(d):
            if d is None or id(d) in seen:
                return
            seen.add(id(d))
            op.deps.append(d)

        for r in reads:
            dep(r.w)
            if r.excl:
                for rd in r.rs:
                    if rd.eng != eng:
                        dep(rd)
        for w in writes:
            dep(w.w)
            for rd in w.rs:
                if rd.eng == eng and rd.semkey is None and dma is None:
                    continue
                dep(rd)
        for r in reads:
            if dma is None:
                r.rs = [x for x in r.rs if x.eng != eng or x.semkey is not None]
            r.rs.append(op)
        for w in writes:
            w.w = op
            w.rs = []
        self.q[eng].append(op)
        return op

    def finalize(self):
        for e in self.ENGS:
            for op in self.q[e]:
                for d in op.deps:
                    if d.semkey is None and d.eng == "pe" and op.eng == "pe" and op.semkey is None:
                        continue
                    d.needed = True
        for e in self.ENGS:
            c = 0
            for op in self.q[e]:
                if op.semkey is None and op.needed:
                    c += 1
                    op.count = c

    def emit(self, e, eng, sems):
        waited = {}
        for op in self.q[e]:
            for d in op.deps:
                if d.semkey is None and d.eng == "pe" and e == "pe" and op.semkey is None:
                    continue
                key = d.semkey if d.semkey is not None else ("E", d.eng)
                if waited.get(key, 0) >= d.count:
                    continue
                eng.wait_ge(sems[key], d.count)
                waited[key] = d.count
            ins = op.fn(eng)
            if op.semkey is not None:
                if not isinstance(ins, (list, tuple)):
                    ins = [ins]
                for i in ins:
                    i.then_inc(sems[op.semkey], 16)
            elif op.needed:
                ins.then_inc(sems[("E", e)], 1)


def build(NSEQ, NTILE, dbg=None, stage=99):
    dbg = dbg or {}
    S = NTILE * 512
    nc = bass.Bass("TRN2", target_bir_lowering=False)
    dram = lambda n, sh, dt, kind="ExternalInput": nc.dram_tensor(n, sh, dt, kind=kind).ap()
    x_d = dram("x", [NSEQ, S, D], F32)
    meta_d = dram("meta", [16, D], F32)
    wts_d = dram("wts", [NUNIT, 128, 4096], F32)
    gam_d = dram("gam", [128, 24], F32)
    gfin_d = dram("gfin", [128, D], F32)
    ssmp_d = dram("ssmp", [128, 3, 16], F32)
    bl_d = dram("bl", [128, 4, 2, 128], F32)
    cl_d = dram("cl", [128, 16, 2, 32], F32)
    dcol_d = dram("dcol", [128, 4], F32)
    sink_d = dram("sinkrow", [1, 2048], F32)
    ident_d = dram("ident", [128, 128], F32)
    maskp_d = dram("maskp", [128, 512], F32)
    maskc_d = dram("maskc", [128, 512], F32)
    oneh_d = dram("onehot", [1, 32], F32)
    y_d = dram("y", [NSEQ, S, D], F32, kind="ExternalOutput")
    wsc_d = dram("wsc", [NUNIT, 128, 4096], BF16, kind="Internal")
    dbg_d = {}
    for name, shape in dbg.items():
        dbg_d[name] = dram("dbg_" + name, list(shape), F32, kind="ExternalOutput")

    sc = Sched()
    es = ExitStack()
    with es:
        def sb(name, shape, dt):
            return es.enter_context(nc.sbuf_tensor("sb_" + name, list(shape), dt))

        h2 = sb("h", [128, 2, 4, D], F32)
        h2_res = [[Res("h%d_%d" % (j, i)) for i in range(4)] for j in range(2)]
        junk_res = Res()
        hn = sb("hn", [128, 2, D], BF16)
        hn_res = [Res(), Res()]
        stat = sb("stat", [128, 16], F32)
        stat_res = [Res() for _ in range(4)]
        eps_t = sb("eps_t", [128, 1], F32)
        eps_res = Res()
        hnT = sb("hnT", [128, 8, 512], BF16)
        hnT_res = Res()
        hnTf = sb("hnTf", [128, 8, 512], BF16)
        hnTf_res = Res()
        wring = sb("wring", [128, NS, 4096], BF16)
        wring_res = [Res("slot%d" % i) for i in range(NS)]
        scrA = sb("scrA", [128, FT * 512], BF16)
        actT = scrA[:, :].rearrange("p (a b) -> p a b", a=FT)
        actT_res = Res()
        mg = scrA[:, 0:4096].bitcast(F32).rearrange("p (a b c) -> p a b c", a=2, b=2)
        sg = scrA[:, 4096:6144].rearrange("p (a b c) -> p a b c", a=2, b=2)
        gl = sb("gl", [128, 1, 2, 512], F32)
        rden = sb("rden", [128, 2, 256], F32)
        sil = sb("sil", [128, 2, 512], BF16)
        sil_res = [Res(), Res()]
        junk = sil[:, :, :].rearrange("p a b -> p (a b)")
        ost_res = [Res(), Res()]
        stg32 = wring[:, 0:4, :].rearrange("p a b -> p (a b)").bitcast(F32).rearrange("p (a b) -> p a b", a=2)
        stg32_res = [Res(), Res()]
        stg16 = scrA[:, 0:8192].rearrange("p (a b) -> p a b", a=2)
        stg16_res = [Res(), Res()]
        ident = sb("ident", [128, 128], BF16)
        maskp = sb("maskp", [128, 512], BF16)
        maskc = sb("maskc", [128, 512], BF16)
        ones = sb("ones", [128, 64], BF16)
        oneh = sb("oneh", [1, 32], BF16)
        sinkb = hn[0:1, :, :].rearrange("p a b -> p (a b)")
        gam = sb("gam", [128, 24], F32)
        gfin = sb("gfin", [128, D], F32)
        dcol = sb("dcol", [128, 4], F32)
        const_res = Res("const")
        ssmp = sb("ssmp", [128, 3, 16], F32)
        sp_t = sb("sp_t", [128, 12, 16], F32)
        Rt = sb("Rt", [128, 16], F32)
        COS = sb("COS", [128, 16, 129], F32)
        SIN = sb("SIN", [128, 16, 129], F32)
        Blb = sb("Blb", [128, 4, 2, 128], BF16)
        Clb = sb("Clb", [128, 16, 3, 32], BF16)
        ST = sb("ST", [128, 16, 2], F32)
        ST0 = sb("ST0", [128, 16, 2], F32)
        ST_res = [Res() for _ in range(16)]
        ST0_res = Res()
        qT = sb("qT", [128, 8, 512], BF16)
        qT_res = Res()
        ostage = qT[:, :, :].rearrange("p a b -> p (a b)").bitcast(F32).rearrange("p (a b) -> p a b", a=2)
        cl32 = qT[:, 0:4, :].rearrange("p a b -> p (a b)").bitcast(F32).rearrange("p (a b c) -> p a b c", a=16, b=2)
        clt = hnT[:, :, :].rearrange("p a b -> p (a b)").bitcast(F32).rearrange("p (a b c) -> p a b c", a=4, b=16)
        kk = sb("kk", [128, 4, 2, 16 + 5 * 128], BF16)
        kk_res = [Res() for _ in range(6)]
        vt = sb("vt", [128, 5, 4, 192], BF16)
        vt_res = [Res() for _ in range(5)]
        vmeta = sb("vmeta", [32, 4, 192], BF16)
        vmeta_res = Res()
        Em = sb("Em", [32, 4, 512], BF16)
        Em_res = [Res() for _ in range(4)]
        Eb = sb("Eb", [128, 4, 512], BF16)
        Eb_res = [Res() for _ in range(4)]
        attnT = sb("attnT", [128, 8, 512], BF16)
        attnT_res = [Res() for _ in range(8)]
        rden_res = [Res(), Res()]
        uTb = sb("uTb", [128, 4, 512], BF16)
        uT_res = Res()
        tcts = sb("tcts", [128, 1024], F32)
        TC = tcts[:, 0:512].rearrange("p (a b c) -> p a b c", a=2, b=2)
        TS = tcts[:, 512:1024].rearrange("p (a b c) -> p a b c", a=2, b=2)
        sbt = tcts[:, 0:1024].rearrange("p (a b) -> p a b", a=2)
        Wt = sb("Wt", [128, 2, 2, 128], F32)
        Zt = sb("Zt", [128, 2, 2, 128], F32)
        QC = sb("QC", [128, 2, 2, 128], BF16)
        QS = sb("QS", [128, 2, 2, 128], BF16)
        zl = sb("zl", [128, 16, 2], F32)
        zl_res = Res()
        ct4 = sb("ct4", [128, 4, 16], F32)
        ct_res = Res()
        ssm_res = {n: [Res(), Res()] for n in ("TC", "TS", "W", "Z", "QC", "QS")}
        yf = sb("yf", [128, 1, 512], F32)
        yf_res = [Res(), Res()]
        gl_res = [Res(), Res()]
        ygT = sb("ygT", [128, 4, 512], BF16)
        ygT_res = Res()
        mg_res = [Res(), Res()]
        sg_res = [Res(), Res()]
        sbt_res = [Res(), Res()]

        NPS = 8
        ps_t = [es.enter_context(nc.psum_tensor("ps%d" % i, [128, 512], F32)) for i in range(NPS)]
        ps_res = [Res("ps%d" % i, excl=True) for i in range(NPS)]
        ps_free = list(range(NPS))

        def alloc_ps():
            return ps_free.pop(0)

        def rel_ps(i):
            ps_free.append(i)

        def PE(fn, reads, writes):
            return sc.add("pe", fn, reads, writes)

        def ACT(fn, reads, writes):
            return sc.add("act", fn, reads, writes)

        def DVE(fn, reads, writes):
            return sc.add("dve", fn, reads, writes)

        def POOL(fn, reads, writes):
            return sc.add("pool", fn, reads, writes)

        def mm(out, lhsT, rhs, start, stop, reads, writes, tp=None, skip=False):
            kw = {}
            if tp is not None:
                kw["tile_position"] = tp
            if skip:
                kw["skip_group_check"] = True
            PE(lambda e: e.matmul(out, lhsT=lhsT, rhs=rhs, start=start, stop=stop, **kw), reads, writes)

        plan = []
        state = {"added": 0, "done": -1, "next": 0}
        wsc_res = [Res("wsc%d" % i) for i in range(NUNIT)]

        def make_plan():
            def ffn(f, npass):
                for u in range(11):
                    plan.append((U_W13[f] + u, 4096))
                for _ in range(npass):
                    for u in range(6):
                        plan.append((U_W2[f] + u, 4096 if u < 5 else 2048))
            ffn(0, 1)
            for i in (0, 1, 4):
                plan.append((U_WIN + i, 4096))
            ntl = NSEQ * NTILE
            for g in range(ntl):
                ffn(0, 1)
                for i in range(5):
                    plan.append((U_WIN + i, 4096))
                if g > 0:
                    ffn(1, 2)
                for m4 in range(4):
                    plan.append((U_WIN + 5 + m4, 4096))
                    plan.append((U_GLU + m4, 2048))
                plan.append((U_WOUT, 4096))
                plan.append((U_WOUT + 1, 4096))
            ffn(1, 2)

        make_plan()

        def prefetch():
            while state["added"] < len(plan) and state["added"] <= state["done"] + NS:
                r = state["added"]
                unit, ncol = plan[r]
                slot = r % NS
                sc.add("sp", (lambda e, slot=slot, unit=unit, ncol=ncol:
                              e.dma_start(out=wring[:, slot, 0:ncol], in_=wsc_d[unit, :, 0:ncol])),
                       reads=[wsc_res[unit]], writes=[wring_res[slot]], dma="ring%d" % slot)
                state["added"] += 1

        def get(unit):
            r = state["next"]
            assert plan[r][0] == unit, (r, plan[r], unit)
            state["next"] += 1
            prefetch()
            assert state["added"] > r
            return r % NS, r

        def done(r):
            assert r == state["done"] + 1, (r, state["done"])
            state["done"] = r
            prefetch()

        def ld(eng, out, in_, key, writes):
            sc.add(eng, lambda e: e.dma_start(out=out, in_=in_), reads=[], writes=writes, dma=key)

        cr = [const_res]
        ld("pool", ident[:], ident_d, "c0", cr)
        ld("pool", maskp[:], maskp_d, "c0", cr)
        ld("pool", maskc[:], maskc_d, "c0", cr)
        ld("pool", oneh[:], oneh_d, "c0", cr)
        ld("pool", sinkb, sink_d, "c0", cr + hn_res)
        ld("pool", Blb[:], bl_d, "c0", cr)
        ld("sp", gam[:], gam_d, "c1", cr)
        ld("sp", gfin[:], gfin_d, "c1", cr)
        ld("sp", dcol[:], dcol_d, "c1", cr)
        ld("sp", ssmp[:], ssmp_d, "c1", cr)
        ld("sp", cl32, cl_d, "c1", cr + [qT_res])
        DVE(lambda e: e.memset(ones[:], 1.0), [], cr)
        DVE(lambda e: e.memset(eps_t[:], EPS), [], [eps_res])
        DVE(lambda e: e.memset(vmeta[:], 1.0), [], [vmeta_res])
        DVE(lambda e: e.memset(vmeta[:, :, 64:128], 0.0), [], [vmeta_res])
        DVE(lambda e: e.memset(vt[:], 1.0), [], vt_res)
        POOL(lambda e: e.memset(kk[:], 0.0), [], kk_res)
        DVE(lambda e: e.memset(ST[:], 0.0), [], ST_res)

        ssm_p_res = Res("ssmprep")
        spr = [ssm_p_res]

        def T_(i):
            return sp_t[:, i, :]

        ar = ssmp[:, 0, :]
        ai = ssmp[:, 1, :]
        ls = ssmp[:, 2, :]
        ACT(lambda e: e.activation(out=T_(0), in_=ls, func=AF.Exp), cr, spr)
        DVE(lambda e: e.tensor_tensor(out=T_(1), in0=ar, in1=T_(0), op=ALU.mult), cr + spr, spr)
        ACT(lambda e: e.activation(out=Rt[:], in_=T_(1), func=AF.Exp), spr, spr)
        DVE(lambda e: e.tensor_tensor(out=T_(2), in0=ai, in1=T_(0), op=ALU.mult), cr + spr, spr)
        DVE(lambda e: e.tensor_scalar(out=T_(2), in0=T_(2), scalar1=1.0 / (2 * np.pi), scalar2=None,
                                      op0=ALU.mult), spr, spr)
        for thr in (1.0, 1.0, 1.0, 1.0, 0.5):
            DVE(lambda e, thr=thr: e.tensor_scalar(out=T_(3), in0=T_(2), scalar1=thr, scalar2=None,
                                                    op0=ALU.is_ge), spr, spr)
            DVE(lambda e: e.tensor_tensor(out=T_(2), in0=T_(2), in1=T_(3), op=ALU.subtract), spr, spr)
        ACT(lambda e: e.activation(out=T_(4), in_=T_(2), func=AF.Sin, scale=6.28318), spr, spr)
        ACT(lambda e: e.activation(out=T_(5), in_=T_(2), func=AF.Sin, scale=3.14159), spr, spr)
        DVE(lambda e: e.tensor_tensor(out=T_(5), in0=T_(5), in1=T_(5), op=ALU.mult), spr, spr)
        DVE(lambda e: e.tensor_scalar(out=T_(5), in0=T_(5), scalar1=-2.0, scalar2=1.0, op0=ALU.mult,
                                      op1=ALU.add), spr, spr)
        tab_res = Res("tables")
        tr = [tab_res]
        DVE(lambda e: e.memset(COS[:, :, 0:1], 1.0), [], tr)
        DVE(lambda e: e.memset(SIN[:, :, 0:1], 0.0), [], tr)
        DVE(lambda e: e.tensor_copy(out=COS[:, :, 1:2], in_=sp_t[:, 5:6, :].rearrange("p a b -> p b a")), spr, tr)
        DVE(lambda e: e.tensor_copy(out=SIN[:, :, 1:2], in_=sp_t[:, 4:5, :].rearrange("p a b -> p b a")), spr, tr)
        ta = actT[:, 0:16, :].bitcast(F32)
        m = 1
        while m < 128:
            cm = COS[:, :, m:m + 1].to_broadcast([128, 16, m])
            sm = SIN[:, :, m:m + 1].to_broadcast([128, 16, m])
            c_in = COS[:, :, 1:m + 1]
            s_in = SIN[:, :, 1:m + 1]
            t1 = ta[:, :, 0:m]
            t2 = ta[:, :, 64:64 + m]
            DVE(lambda e, c_in=c_in, cm=cm, t1=t1: e.tensor_tensor(out=t1, in0=c_in, in1=cm, op=ALU.mult), tr, [actT_res])
            DVE(lambda e, s_in=s_in, sm=sm, t2=t2: e.tensor_tensor(out=t2, in0=s_in, in1=sm, op=ALU.mult), tr, [actT_res])
            DVE(lambda e, m=m, t1=t1, t2=t2: e.tensor_tensor(out=COS[:, :, m + 1:2 * m + 1], in0=t1, in1=t2,
                                                             op=ALU.subtract), [actT_res, tab_res], tr)
            t3 = ta[:, :, 128:128 + m]
            t4 = ta[:, :, 192:192 + m]
            DVE(lambda e, c_in=c_in, sm=sm, t3=t3: e.tensor_tensor(out=t3, in0=c_in, in1=sm, op=ALU.mult), tr, [actT_res])
            DVE(lambda e, s_in=s_in, cm=cm, t4=t4: e.tensor_tensor(out=t4, in0=s_in, in1=cm, op=ALU.mult), tr, [actT_res])
            DVE(lambda e, m=m, t3=t3, t4=t4: e.tensor_tensor(out=SIN[:, :, m + 1:2 * m + 1], in0=t3, in1=t4,
                                                             op=ALU.add), [actT_res, tab_res], tr)
            m *= 2
        DVE(lambda e: e.tensor_tensor(out=T_(6), in0=Rt[:], in1=T_(5), op=ALU.mult), spr, spr)
        DVE(lambda e: e.tensor_tensor(out=T_(7), in0=Rt[:], in1=T_(4), op=ALU.mult), spr, spr)
        DVE(lambda e: e.tensor_scalar(out=T_(6), in0=T_(6), scalar1=-1.0, scalar2=None, op0=ALU.add), spr, spr)
        DVE(lambda e: e.tensor_tensor(out=T_(8), in0=ar, in1=ar, op=ALU.mult), cr + spr, spr)
        DVE(lambda e: e.tensor_tensor(out=T_(9), in0=ai, in1=ai, op=ALU.mult), cr + spr, spr)
        DVE(lambda e: e.tensor_tensor(out=T_(8), in0=T_(8), in1=T_(9), op=ALU.add), spr, spr)
        DVE(lambda e: e.reciprocal(out=T_(8), in_=T_(8)), spr, spr)
        DVE(lambda e: e.tensor_tensor(out=T_(9), in0=T_(6), in1=ar, op=ALU.mult), cr + spr, spr)
        DVE(lambda e: e.tensor_tensor(out=T_(10), in0=T_(7), in1=ai, op=ALU.mult), cr + spr, spr)
        DVE(lambda e: e.tensor_tensor(out=T_(9), in0=T_(9), in1=T_(10), op=ALU.add), spr, spr)
        DVE(lambda e: e.tensor_tensor(out=T_(9), in0=T_(9), in1=T_(8), op=ALU.mult), spr, spr)
        DVE(lambda e: e.tensor_tensor(out=T_(10), in0=T_(7), in1=ar, op=ALU.mult), cr + spr, spr)
        DVE(lambda e: e.tensor_tensor(out=T_(11), in0=T_(6), in1=ai, op=ALU.mult), cr + spr, spr)
        DVE(lambda e: e.tensor_tensor(out=T_(10), in0=T_(10), in1=T_(11), op=ALU.subtract), spr, spr)
        DVE(lambda e: e.tensor_tensor(out=T_(10), in0=T_(10), in1=T_(8), op=ALU.mult), spr, spr)
        cre = cl32[:, :, 0, :]
        cim = cl32[:, :, 1, :]
        kre = sp_t[:, 9:10, :].rearrange("p a b -> p b a").to_broadcast([128, 16, 32])
        kim = sp_t[:, 10:11, :].rearrange("p a b -> p b a").to_broadcast([128, 16, 32])
        clr = Res("clb")
        crq = cr + spr + [qT_res]
        DVE(lambda e: e.tensor_tensor(out=clt[:, 0], in0=cre, in1=kre, op=ALU.mult), crq, [hnT_res])
        DVE(lambda e: e.tensor_tensor(out=clt[:, 1], in0=cim, in1=kim, op=ALU.mult), crq, [hnT_res])
        DVE(lambda e: e.tensor_tensor(out=clt[:, 2], in0=cre, in1=kim, op=ALU.mult), crq, [hnT_res])
        DVE(lambda e: e.tensor_tensor(out=clt[:, 3], in0=cim, in1=kre, op=ALU.mult), crq, [hnT_res])
        DVE(lambda e: e.tensor_tensor(out=Clb[:, :, 0, :], in0=clt[:, 0], in1=clt[:, 1], op=ALU.subtract), [hnT_res], [clr])
        DVE(lambda e: e.tensor_tensor(out=Clb[:, :, 1, :], in0=clt[:, 1], in1=clt[:, 0], op=ALU.subtract), [hnT_res], [clr])
        DVE(lambda e: e.tensor_tensor(out=clt[:, 2], in0=clt[:, 2], in1=clt[:, 3], op=ALU.add), [hnT_res], [hnT_res])
        DVE(lambda e: e.tensor_scalar(out=Clb[:, :, 2, :], in0=clt[:, 2], scalar1=-1.0, scalar2=None,
                                      op0=ALU.mult), [hnT_res], [clr])
        for kh in range(4):
            pi = alloc_ps()
            mm(ps_t[pi][0:32, :], oneh[0:1, :], sinkb[0:1, kh * 512:(kh + 1) * 512], True, True, cr + hn_res, [ps_res[pi]])
            ACT(lambda e, pi=pi, kh=kh: e.activation(out=Em[:, kh, :], in_=ps_t[pi][0:32, :], func=AF.Exp),
                [ps_res[pi]], [Em_res[kh]])
            rel_ps(pi)

        first_use = []
        seen_u = set()
        for (u, _) in plan:
            if u not in seen_u:
                seen_u.add(u)
                first_use.append(u)
        assert len(first_use) == NUNIT

        def gam_col(u):
            if U_W13[0] <= u < U_W13[0] + 11:
                return 0, "w13"
            if U_W13[1] <= u < U_W13[1] + 11:
                return 16, "w13"
            if U_WIN <= u < U_WIN + 9:
                return 8, "win"
            return None, None

        for n, u in enumerate(first_use):
            b = n % 2
            w32 = [stg32_res[b], wring_res[2 * b], wring_res[2 * b + 1]]
            w16 = [stg16_res[b], actT_res]
            sc.add("sp", lambda e, u=u, b=b: e.dma_start(out=stg32[:, b, :], in_=wts_d[u]), reads=[],
                   writes=w32, dma="stg%d" % b)
            gc, kind = gam_col(u)
            if kind is None:
                for hh in range(2):
                    if hh == 0:
                        DVE(lambda e, b=b, hh=hh: e.tensor_copy(out=stg16[:, b, hh * 2048:(hh + 1) * 2048],
                                                                in_=stg32[:, b, hh * 2048:(hh + 1) * 2048]), w32, w16)
                    else:
                        ACT(lambda e, b=b, hh=hh: e.copy(out=stg16[:, b, hh * 2048:(hh + 1) * 2048],
                                                         in_=stg32[:, b, hh * 2048:(hh + 1) * 2048]), w32, w16)
            else:
                for k in range(8):
                    if kind == "w13":
                        src = stg32[:, b, :].rearrange("p (a k c) -> p a k c", a=4, k=8)[:, :, k, :]
                        dst = stg16[:, b, :].rearrange("p (a k c) -> p a k c", a=4, k=8)[:, :, k, :]
                    else:
                        src = stg32[:, b, k * 512:(k + 1) * 512]
                        dst = stg16[:, b, k * 512:(k + 1) * 512]
                    if k % 2 == 0:
                        DVE(lambda e, src=src, dst=dst, gc=gc, k=k: e.tensor_scalar(
                            out=dst, in0=src, scalar1=gam[:, gc + k:gc + k + 1], scalar2=None, op0=ALU.mult),
                            w32 + cr, w16)
                    else:
                        ACT(lambda e, src=src, dst=dst, gc=gc, k=k: e.mul(out=dst, in_=src, mul=gam[:, gc + k:gc + k + 1]),
                            w32 + cr, w16)
            sc.add("sp", lambda e, u=u, b=b: e.dma_start(out=wsc_d[u], in_=stg16[:, b, :]),
                   reads=w16, writes=[wsc_res[u]], dma="sto%d" % b)

        def norm_T(nblk, ntok, hs, hT, hT_res):
            for b in range(nblk):
                hb = h2[:ntok, hs, b, :]
                hr = h2_res[hs][b]
                sr = stat_res[b]
                ACT(lambda e, hb=hb, b=b: e.activation(out=junk[:ntok, :], in_=hb, func=AF.Square,
                                                       accum_out=stat[:ntok, b:b + 1]),
                    [hr], [junk_res, sr] + sil_res)
                ACT(lambda e, b=b: e.activation(out=stat[:ntok, 4 + b:5 + b], in_=stat[:ntok, b:b + 1], func=AF.Sqrt,
                                                bias=eps_t[:ntok, 0:1], scale=1.0 / D), [sr, eps_res], [sr])
                DVE(lambda e, b=b: e.reciprocal(out=stat[:ntok, 8 + b:9 + b], in_=stat[:ntok, 4 + b:5 + b]), [sr], [sr])
                hb_i = b % 2
                DVE(lambda e, hb=hb, b=b, hb_i=hb_i: e.tensor_scalar(out=hn[:ntok, hb_i, :], in0=hb,
                                                                    scalar1=stat[:ntok, 8 + b:9 + b], scalar2=None,
                                                                    op0=ALU.mult),
                    [hr, sr], [hn_res[hb_i]])
                pi = alloc_ps()
                pv = ps_t[pi][:, :].bitcast(BF16)
                for k in range(8):
                    PE(lambda e, pv=pv, k=k, hb_i=hb_i: e.transpose(
                        out=pv[:, k * ntok:(k + 1) * ntok], in_=hn[:ntok, hb_i, k * 128:(k + 1) * 128],
                        identity=ident[:ntok, :ntok]), [hn_res[hb_i], const_res], [ps_res[pi]])
                for half in range(2):
                    src = pv[:, half * 4 * ntok:(half + 1) * 4 * ntok].rearrange("p (a t) -> p a t", a=4)
                    dst = hT[:, half * 4:(half + 1) * 4, b * ntok:(b + 1) * ntok]
                    if half == 0:
                        ACT(lambda e, src=src, dst=dst: e.copy(out=dst, in_=src), [ps_res[pi]], [hT_res])
                    else:
                        DVE(lambda e, src=src, dst=dst: e.tensor_copy(out=dst, in_=src), [ps_res[pi]], [hT_res])
                rel_ps(pi)
                yield "norm"

        def ffn_gen(f, hs, nblk, ntok, npass_req=2):
            TW = nblk * ntok
            yield from norm_T(nblk, ntok, hs, hnTf, hnTf_res)
            for u in range(11):
                slot, r = get(U_W13[f] + u)
                wr = wring_res[slot]
                for j2 in range(2):
                    j = 2 * u + j2
                    pa = alloc_ps()
                    pb = alloc_ps()
                    for s_, pi in ((0, pa), (1, pb)):
                        for k in range(8):
                            off = ((j2 * 2 + s_) * 8 + k) * 128
                            mm(ps_t[pi][:, :TW], wring[:, slot, off:off + 128], hnTf[:, k, :TW], k == 0, k == 7,
                               [wr, hnTf_res], [ps_res[pi]])
                        if s_ == 0:
                            yield "up"
                    si = j % 2
                    ACT(lambda e, pa=pa, si=si: e.activation(out=sil[:, si, :TW], in_=ps_t[pa][:, :TW], func=AF.Silu),
                        [ps_res[pa]], [sil_res[si]])
                    DVE(lambda e, pb=pb, si=si, j=j: e.tensor_tensor(out=actT[:, j, :TW], in0=sil[:, si, :TW],
                                                                    in1=ps_t[pb][:, :TW], op=ALU.mult),
                        [ps_res[pb], sil_res[si]], [actT_res])
                    rel_ps(pa)
                    rel_ps(pb)
                    if j2 == 1:
                        done(r)
                    yield "up"
            npass = npass_req if nblk == 4 else 1
            for pss in range(npass):
                blks = ([2 * pss, 2 * pss + 1] if npass == 2 else [0, 1, 2, 3]) if nblk == 4 else [0]
                banks = {}
                for b in blks:
                    for half in range(2):
                        banks[(b, half)] = alloc_ps()
                for u in range(6):
                    slot, r = get(U_W2[f] + u)
                    for b in blks:
                        for half in range(2):
                            pi = banks[(b, half)]
                            for k4 in range(4):
                                k = 4 * u + k4
                                if k >= FT:
                                    continue
                                off = k4 * 1024 + half * 512
                                mm(ps_t[pi][:ntok, :], actT[:, k, b * ntok:(b + 1) * ntok], wring[:, slot, off:off + 512],
                                   k == 0, k == FT - 1, [wring_res[slot], actT_res], [ps_res[pi]])
                    done(r)
                    yield "down"
                for (b, half), pi in banks.items():
                    DVE(lambda e, pi=pi, b=b, half=half: e.scalar_tensor_tensor(
                        out=h2[:ntok, hs, b, half * 512:(half + 1) * 512], in0=ps_t[pi][:ntok, :], scalar=0.5,
                        in1=h2[:ntok, hs, b, half * 512:(half + 1) * 512], op0=ALU.mult, op1=ALU.add),
                        [ps_res[pi], h2_res[hs][b]], [h2_res[hs][b]])
                    rel_ps(pi)
                yield "down"

        def final_gen(hs, s, t):
            for b in range(4):
                sr = stat_res[b]
                hr = h2_res[hs][b]
                ACT(lambda e, b=b: e.activation(out=junk[:, :], in_=h2[:, hs, b, :], func=AF.Square,
                                                accum_out=stat[:, b:b + 1]), [hr], [junk_res, sr] + sil_res)
                ACT(lambda e, b=b: e.activation(out=stat[:, 4 + b:5 + b], in_=stat[:, b:b + 1], func=AF.Sqrt,
                                                bias=eps_t[:, 0:1], scale=1.0 / D), [sr, eps_res], [sr])
                DVE(lambda e, b=b: e.reciprocal(out=stat[:, 8 + b:9 + b], in_=stat[:, 4 + b:5 + b]), [sr], [sr])
                oi = b % 2
                DVE(lambda e, b=b, oi=oi: e.scalar_tensor_tensor(
                    out=ostage[:, oi, :], in0=h2[:, hs, b, :], scalar=stat[:, 8 + b:9 + b], in1=gfin[:], op0=ALU.mult,
                    op1=ALU.mult), [hr, sr, const_res], [ost_res[oi], qT_res])
                r0 = t * 512 + b * 128
                sc.add("pool", lambda e, s=s, r0=r0, oi=oi: e.dma_start(out=y_d[s, r0:r0 + 128, :], in_=ostage[:, oi, :]),
                       reads=[ost_res[oi], qT_res], writes=[Res()], dma="yo%d" % oi)
                yield "final"

        def run_all(gen):
            for _ in gen:
                pass

        def proj_fm(slot, coff, ncols_tile, TW, evac):
            pi = alloc_ps()
            for k in range(8):
                mm(ps_t[pi][:, :TW], wring[:, slot, k * 512 + coff:k * 512 + coff + 128], hnT[:, k, :TW], k == 0, k == 7,
                   [wring_res[slot], hnT_res], [ps_res[pi]])
            evac(pi)
            rel_ps(pi)

        def ssm_chunk(L, cols, meta, seq_first, yfi, inject=None):
            if seq_first and not meta:
                DVE(lambda e: e.tensor_copy(out=ST[:], in_=ST0[:]), [ST0_res], ST_res)
            py = None
            if not meta:
                py = alloc_ps()

            def S1(p):
                ct, q = p // 4, p % 4
                bi = p % 2
                px = alloc_ps()
                for part in range(2):
                    mm(ps_t[px][:, part * 128:part * 128 + L], Blb[32 * q:32 * q + 32, ct, part, :],
                       uTb[32 * q:32 * q + 32, ct, cols], True, True, [const_res, uT_res], [ps_res[px]], tp=(32 * q, 0))
                pxv = ps_t[px][:, 0:256].rearrange("p (a t) -> p a t", a=2)[:, :, 0:L]
                cosb = COS[:, p:p + 1, 0:L].to_broadcast([128, 2, L])
                sinb = SIN[:, p:p + 1, 0:L].to_broadcast([128, 2, L])
                DVE(lambda e: e.tensor_tensor(out=TC[:, bi, :, 0:L], in0=pxv, in1=cosb, op=ALU.mult),
                    [ps_res[px], tab_res], [ssm_res["TC"][bi]])
                DVE(lambda e: e.tensor_tensor(out=TS[:, bi, :, 0:L], in0=pxv, in1=sinb, op=ALU.mult),
                    [ps_res[px], tab_res], [ssm_res["TS"][bi]])
                rel_ps(px)

            def S2(p):
                bi = p % 2
                POOL(lambda e: e.tensor_tensor(out=Wt[:, bi, 0, 0:L], in0=TC[:, bi, 0, 0:L], in1=TS[:, bi, 1, 0:L],
                                               op=ALU.add),
                     [ssm_res["TC"][bi], ssm_res["TS"][bi]], [ssm_res["W"][bi]])
                POOL(lambda e: e.tensor_tensor(out=Wt[:, bi, 1, 0:L], in0=TC[:, bi, 1, 0:L], in1=TS[:, bi, 0, 0:L],
                                               op=ALU.subtract),
                     [ssm_res["TC"][bi], ssm_res["TS"][bi]], [ssm_res["W"][bi]])

            def S3(p):
                bi = p % 2
                for part in range(2):
                    DVE(lambda e, part=part: e.tensor_tensor_scan(
                        out=Zt[:, bi, part, 0:L], data0=Rt[:, p:p + 1].to_broadcast([128, L]),
                        data1=Wt[:, bi, part, 0:L], initial=ST[:, p, part:part + 1], op0=ALU.mult, op1=ALU.add),
                        [ssm_res["W"][bi], ST_res[p], ssm_p_res], [ssm_res["Z"][bi]])
                DVE(lambda e: e.tensor_copy(out=zl[:, p, :], in_=Zt[:, bi, :, L - 1]), [ssm_res["Z"][bi]], [zl_res])

            def S4(p):
                bi = p % 2
                cosb2 = COS[:, p:p + 1, 0:L].to_broadcast([128, 2, L])
                sinb2 = SIN[:, p:p + 1, 0:L].to_broadcast([128, 2, L])
                POOL(lambda e: e.tensor_tensor(out=QC[:, bi, :, 0:L], in0=Zt[:, bi, :, 0:L], in1=cosb2, op=ALU.mult),
                     [ssm_res["Z"][bi], tab_res], [ssm_res["QC"][bi]])
                POOL(lambda e: e.tensor_tensor(out=QS[:, bi, :, 0:L], in0=Zt[:, bi, :, 0:L], in1=sinb2, op=ALU.mult),
                     [ssm_res["Z"][bi], tab_res], [ssm_res["QS"][bi]])

            def S5(p):
                ct, q = p // 4, p % 4
                bi = p % 2
                yo = ps_t[py][32 * q:32 * q + 32, ct * 128:ct * 128 + L]
                rq = [ssm_res["QC"][bi], ssm_res["QS"][bi], clr]
                mm(yo, Clb[:, p, 0, :], QC[:, bi, 0, 0:L], True, False, rq, [ps_res[py]], tp=(0, 32 * q))
                mm(yo, Clb[:, p, 1, :], QS[:, bi, 1, 0:L], False, False, rq, [ps_res[py]], tp=(0, 32 * q))
                mm(yo, Clb[:, p, 2, :], QS[:, bi, 0, 0:L], False, False, rq, [ps_res[py]], tp=(0, 32 * q))
                mm(yo, Clb[:, p, 2, :], QC[:, bi, 1, 0:L], False, True, rq, [ps_res[py]], tp=(0, 32 * q))

            for i in range(19):
                if i < 16:
                    S1(i)
                if 1 <= i <= 16:
                    S2(i - 1)
                if 2 <= i <= 17 and not meta:
                    S4(i - 2)
                if 1 <= i <= 16:
                    S3(i - 1)
                if 3 <= i <= 18 and not meta:
                    S5(i - 3)
                if inject is not None:
                    inject(i)
            cLv = COS[:, :, L]
            sLv = SIN[:, :, L]
            zr = zl[:, :, 0]
            zi = zl[:, :, 1]
            rr = [zl_res, tab_res]
            DVE(lambda e: e.tensor_tensor(out=ct4[:, 0, :], in0=zr, in1=cLv, op=ALU.mult), rr, [ct_res])
            DVE(lambda e: e.tensor_tensor(out=ct4[:, 1, :], in0=zi, in1=sLv, op=ALU.mult), rr, [ct_res])
            DVE(lambda e: e.tensor_tensor(out=ct4[:, 2, :], in0=zr, in1=sLv, op=ALU.mult), rr, [ct_res])
            DVE(lambda e: e.tensor_tensor(out=ct4[:, 3, :], in0=zi, in1=cLv, op=ALU.mult), rr, [ct_res])
            DVE(lambda e: e.tensor_tensor(out=ST[:, :, 0], in0=ct4[:, 0, :], in1=ct4[:, 1, :], op=ALU.subtract),
                [ct_res], ST_res)
            DVE(lambda e: e.tensor_tensor(out=ST[:, :, 1], in0=ct4[:, 2, :], in1=ct4[:, 3, :], op=ALU.add),
                [ct_res], ST_res)
            if meta:
                return
            for ct in range(4):
                DVE(lambda e, ct=ct: e.scalar_tensor_tensor(
                    out=yf[:, yfi, ct * 128:(ct + 1) * 128], in0=uTb[:, ct, cols], scalar=dcol[:, ct:ct + 1],
                    in1=ps_t[py][:, ct * 128:(ct + 1) * 128], op0=ALU.mult, op1=ALU.add),
                    [ps_res[py], uT_res, const_res], [yf_res[yfi]])
            rel_ps(py)
            g0 = gl[:, yfi, 0, :]
            g1 = gl[:, yfi, 1, :]
            yv = yf[:, yfi, :]
            POOL(lambda e: e.tensor_tensor(out=g0, in0=yv, in1=yv, op=ALU.mult), [yf_res[yfi], actT_res], [gl_res[yfi]])
            POOL(lambda e: e.tensor_scalar(out=g0, in0=g0, scalar1=0.044715, scalar2=1.0, op0=ALU.mult, op1=ALU.add),
                 [gl_res[yfi], actT_res], [gl_res[yfi]])
            POOL(lambda e: e.tensor_tensor(out=g0, in0=g0, in1=yv, op=ALU.mult), [gl_res[yfi], yf_res[yfi], actT_res],
                 [gl_res[yfi]])
            ACT(lambda e: e.activation(out=g1, in_=g0, func=AF.Sigmoid, scale=GELU_C), [gl_res[yfi], actT_res],
                [gl_res[yfi]])
            dst = ygT[:, :, cols]
            DVE(lambda e: e.tensor_tensor(out=dst, in0=yv.rearrange("p (a t) -> p a t", a=4), in1=g1.rearrange(
                "p (a t) -> p a t", a=4), op=ALU.mult), [gl_res[yfi], yf_res[yfi], actT_res], [ygT_res])

        def dump(name, src_ap, reads, idx=None):
            if name not in dbg_d:
                return
            dst = dbg_d[name] if idx is None else dbg_d[name][idx]
            n = state.setdefault("ndbg", 0)
            state["ndbg"] = n + 1
            sc.add("pool", lambda e: e.dma_start(out=dst, in_=src_ap), reads=reads, writes=[Res()], dma="dbg%d" % n)

        sc.add("pool", lambda e: e.dma_start(out=h2[0:16, 0, 0, :], in_=meta_d), reads=[], writes=[h2_res[0][0]], dma="xl0")
        run_all(ffn_gen(0, 0, 1, 16))
        run_all(norm_T(1, 16, 0, hnT, hnT_res))
        slot, r = get(U_WIN + 0)
        for kh in range(4):
            def ev(pi, kh=kh):
                ACT(lambda e: e.copy(out=kk[0:64, kh, 0, 0:16], in_=ps_t[pi][0:64, 0:16]), [ps_res[pi]], [kk_res[0]])
                ACT(lambda e: e.copy(out=kk[64:128, kh, 1, 0:16], in_=ps_t[pi][64:128, 0:16]), [ps_res[pi]], [kk_res[0]])
            proj_fm(slot, kh * 128, 128, 16, ev)
        done(r)
        slot, r = get(U_WIN + 1)
        pi = alloc_ps()
        for k in range(8):
            mm(ps_t[pi][0:16, 0:256], hnT[:, k, 0:16], wring[:, slot, k * 512:k * 512 + 256], k == 0, k == 7,
               [wring_res[slot], hnT_res], [ps_res[pi]])
        ACT(lambda e, pi=pi: e.copy(out=vmeta[0:16, :, 64:128], in_=ps_t[pi][0:16, 0:256].rearrange("p (a d) -> p a d", a=4)),
            [ps_res[pi]], [vmeta_res])
        rel_ps(pi)
        done(r)
        slot, r = get(U_WIN + 4)
        for ct in range(4):
            def ev(pi, ct=ct):
                DVE(lambda e: e.tensor_copy(out=uTb[:, ct, 0:16], in_=ps_t[pi][:, 0:16]), [ps_res[pi]], [uT_res])
            proj_fm(slot, ct * 128, 128, 16, ev)
        done(r)
        ssm_chunk(16, slice(0, 16), True, True, 0)
        DVE(lambda e: e.tensor_copy(out=ST0[:], in_=ST[:]), ST_res, [ST0_res])

        tiles = [(s, t) for s in range(NSEQ) for t in range(NTILE)]

        def load_x(g):
            s, t = tiles[g]
            hs = g % 2
            for b in range(4):
                r0 = t * 512 + b * 128
                sc.add("pool", lambda e, s=s, r0=r0, b=b, hs=hs: e.dma_start(out=h2[:, hs, b, :], in_=x_d[s, r0:r0 + 128, :]),
                       reads=[], writes=[h2_res[hs][b]], dma="xl%d" % b)

        def SA(b, kh):
            gb = cur['t'] * 4 + b
            qc = slice(b * 128, (b + 1) * 128)
            emi = kh
            p0 = alloc_ps()
            keyt = [("m", p0, None)]
            if gb > 0:
                keyt.append(("p", alloc_ps(), maskp))
            keyt.append(("c", alloc_ps(), maskc))
            ebufs = []
            for kind, pi, msk in keyt:
                if kind == "m":
                    kcols = slice(0, 16)
                    nk = 16
                    kres = kk_res[0]
                else:
                    sl = b if kind == "p" else b + 1
                    kcols = slice(16 + sl * 128, 16 + (sl + 1) * 128)
                    nk = 128
                    kres = kk_res[1 + sl]
                for half in range(2):
                    o = ps_t[pi][0:nk, half * 256:(half + 1) * 256].rearrange("p (a t) -> p a t", a=2)
                    mm(o, kk[:, kh, half, kcols], qT[:, 2 * kh:2 * kh + 2, qc], half == 0, msk is None,
                       [kres, qT_res], [ps_res[pi]], skip=True)
                if msk is not None:
                    mm(ps_t[pi][:, :], ident[:], msk[:], False, True, [const_res], [ps_res[pi]], skip=True)
                if kind == "m":
                    ACT(lambda e, pi=pi, emi=emi: e.activation(out=Em[0:16, emi, :], in_=ps_t[pi][0:16, :],
                                                              func=AF.Exp, scale=0.125),
                        [ps_res[pi]], [Em_res[emi]])
                    ebufs.append((Em[0:17, emi, :], Em_res[emi], 17, vmeta[0:17, kh, :], vmeta_res))
                else:
                    ei = state.setdefault("ei", 0)
                    state["ei"] = (ei + 1) % 4
                    ACT(lambda e, pi=pi, ei=ei: e.activation(out=Eb[:, ei, :], in_=ps_t[pi][:, :],
                                                            func=AF.Exp, scale=0.125),
                        [ps_res[pi]], [Eb_res[ei]])
                    vs = b if kind == "p" else b + 1
                    ebufs.append((Eb[:, ei, :], Eb_res[ei], 128, vt[:, vs, kh, :], vt_res[vs]))
                rel_ps(pi)
            return ebufs

        def SB(b, kh, ebufs):
            qc = slice(b * 128, (b + 1) * 128)
            pn = alloc_ps()
            for idx, (E, Er, K, V, Vr) in enumerate(ebufs):
                sp_ = idx == len(ebufs) - 1
                mm(ps_t[pn][:, 0:256], V[:, 64:192], E[:, 0:256], idx == 0, sp_, [Er, Vr], [ps_res[pn]], skip=True)
                mm(ps_t[pn][:, 256:512], V[:, 0:128], E[:, 256:512], False, sp_, [Er, Vr], [ps_res[pn]], skip=True)
            ri = kh % 2
            ACT(lambda e: e.activation(out=rden[0:64, ri, :], in_=ps_t[pn][64:128, 0:256], func=AF.Ln),
                [ps_res[pn]], [rden_res[ri]])
            ACT(lambda e: e.activation(out=rden[64:128, ri, :], in_=ps_t[pn][0:64, 256:512], func=AF.Ln),
                [ps_res[pn]], [rden_res[ri]])
            ACT(lambda e: e.activation(out=rden[:, ri, :], in_=rden[:, ri, :], func=AF.Exp, scale=-1.0),
                [rden_res[ri]], [rden_res[ri]])
            for half in range(2):
                rows = slice(half * 64, (half + 1) * 64)
                DVE(lambda e, rows=rows, half=half: e.tensor_tensor(
                    out=attnT[rows, 2 * kh:2 * kh + 2, qc],
                    in0=ps_t[pn][rows, half * 256:(half + 1) * 256].rearrange("p (a t) -> p a t", a=2),
                    in1=rden[rows, ri, :].rearrange("p (a t) -> p a t", a=2), op=ALU.mult),
                    [ps_res[pn], rden_res[ri], actT_res], [attnT_res[2 * kh], attnT_res[2 * kh + 1]])
            rel_ps(pn)


        cur = {"t": 0}
        load_x(0)
        prevF = None
        for g, (s, t) in enumerate(tiles):
            first = (g == 0)
            hs = g % 2
            cur["t"] = t
            run_all(ffn_gen(0, hs, 4, 128, npass_req=1))
            if first:
                for b in range(4):
                    dump("h1", h2[:, hs, b, :], [h2_res[hs][b]], idx=b)
            run_all(norm_T(4, 128, hs, hnT, hnT_res))
            slot, r = get(U_WIN + 0)
            for kh in range(4):
                def ev(pi, kh=kh):
                    for hv in range(2):
                        rows = slice(hv * 64, (hv + 1) * 64)
                        if kh % 2 == 0:
                            ACT(lambda e, rows=rows, hv=hv: e.copy(out=kk[rows, kh, hv, 144:656], in_=ps_t[pi][rows, :]),
                                [ps_res[pi]], kk_res[2:6])
                        else:
                            DVE(lambda e, rows=rows, hv=hv: e.tensor_copy(out=kk[rows, kh, hv, 144:656],
                                                                         in_=ps_t[pi][rows, :]),
                                [ps_res[pi]], kk_res[2:6])
                proj_fm(slot, kh * 128, 128, 512, ev)
            done(r)
            slot, r = get(U_WIN + 1)
            for b in range(4):
                pi = alloc_ps()
                for k in range(8):
                    mm(ps_t[pi][:, 0:256], hnT[:, k, b * 128:(b + 1) * 128], wring[:, slot, k * 512:k * 512 + 256],
                       k == 0, k == 7, [wring_res[slot], hnT_res], [ps_res[pi]])
                ACT(lambda e, pi=pi, b=b: e.copy(out=vt[:, 1 + b, :, 64:128],
                                                 in_=ps_t[pi][:, 0:256].rearrange("p (a d) -> p a d", a=4)), [ps_res[pi]],
                    [vt_res[1 + b]])
                rel_ps(pi)
            done(r)
            for qu in range(2):
                slot, r = get(U_WIN + 2 + qu)
                for jj in range(4):
                    j = qu * 4 + jj

                    def ev(pi, j=j):
                        if j % 2 == 0:
                            ACT(lambda e: e.copy(out=qT[:, j, :], in_=ps_t[pi][:, :]), [ps_res[pi]], [qT_res])
                        else:
                            DVE(lambda e: e.tensor_copy(out=qT[:, j, :], in_=ps_t[pi][:, :]), [ps_res[pi]], [qT_res])
                    proj_fm(slot, jj * 128, 128, 512, ev)
                done(r)
            slot, r = get(U_WIN + 4)
            for ct in range(4):
                def ev(pi, ct=ct):
                    if ct % 2 == 0:
                        ACT(lambda e: e.copy(out=uTb[:, ct, :], in_=ps_t[pi][:, :]), [ps_res[pi]], [uT_res])
                    else:
                        DVE(lambda e: e.tensor_copy(out=uTb[:, ct, :], in_=ps_t[pi][:, :]), [ps_res[pi]], [uT_res])
                proj_fm(slot, ct * 128, 128, 512, ev)
            done(r)
            items = [(b, kh) for b in range(4) for kh in range(4)]
            att = {"k": 0, "pend": None}

            def att_step():
                k = att["k"]
                if k < 16:
                    b_, kh_ = items[k]
                    eb = SA(b_, kh_)
                    if att["pend"] is not None:
                        SB(*att["pend"])
                    att["pend"] = (b_, kh_, eb)
                elif k == 16:
                    SB(*att["pend"])
                    att["pend"] = None
                att["k"] = k + 1

            fst = {"gen": prevF, "n": 0, "live": prevF is not None}

            def f_step(allow_down):
                if not fst["live"]:
                    return
                if fst["n"] >= 48 and not allow_down:
                    return
                try:
                    next(fst["gen"])
                    fst["n"] += 1
                except StopIteration:
                    fst["live"] = False

            slot_n = {"n": 0}

            def inj(i):
                n = slot_n["n"]
                slot_n["n"] = n + 1
                if att["k"] <= 16:
                    att_step()
                if ((n + 1) * 67) // 75 > (n * 67) // 75:
                    f_step(att["k"] > 16)

            for c in range(4):
                ssm_chunk(128, slice(c * 128, (c + 1) * 128), False, (t == 0 and c == 0), 0, inject=inj)
            while att["k"] <= 16:
                att_step()
            while fst["live"]:
                f_step(True)
            if first:
                for j in range(8):
                    dump("attn", attnT[:, j, :], [attnT_res[j]], idx=j)
                dump("yg", ygT[:, :, :], [ygT_res])
            if g + 1 < len(tiles):
                load_x(g + 1)
            for m4 in range(4):
                slot, r = get(U_WIN + 5 + m4)
                slotG, rG = get(U_GLU + m4)
                for jj in range(2):
                    j = 2 * m4 + jj
                    bi = j % 2
                    pga = alloc_ps()
                    pgs = alloc_ps()
                    for k in range(8):
                        mm(ps_t[pga][:, :], wring[:, slot, k * 512 + jj * 128:k * 512 + jj * 128 + 128], hnT[:, k, :],
                           k == 0, k == 7, [wring_res[slot], hnT_res], [ps_res[pga]])
                    for k in range(8):
                        mm(ps_t[pgs][:, :], wring[:, slot, k * 512 + 256 + jj * 128:k * 512 + 256 + jj * 128 + 128],
                           hnT[:, k, :], k == 0, k == 7, [wring_res[slot], hnT_res], [ps_res[pgs]])
                    pA = alloc_ps()
                    pB = alloc_ps()
                    for ct in range(4):
                        mm(ps_t[pA][:, :], wring[:, slotG, ct * 512 + jj * 128:ct * 512 + jj * 128 + 128], ygT[:, ct, :],
                           ct == 0, ct == 3, [wring_res[slotG], ygT_res], [ps_res[pA]])
                    for ct in range(4):
                        mm(ps_t[pB][:, :], wring[:, slotG, ct * 512 + 256 + jj * 128:ct * 512 + 256 + jj * 128 + 128],
                           ygT[:, ct, :], ct == 0, ct == 3, [wring_res[slotG], ygT_res], [ps_res[pB]])
                    ACT(lambda e, pga=pga, bi=bi: e.activation(out=sg[:, bi, 0, :], in_=ps_t[pga][:, :],
                                                              func=AF.Sigmoid), [ps_res[pga], actT_res], [sg_res[bi]])
                    ACT(lambda e, pgs=pgs, bi=bi: e.activation(out=sg[:, bi, 1, :], in_=ps_t[pgs][:, :],
                                                              func=AF.Sigmoid), [ps_res[pgs], actT_res], [sg_res[bi]])
                    ACT(lambda e, pB=pB, bi=bi: e.activation(out=sbt[:, bi, :], in_=ps_t[pB][:, :], func=AF.Sigmoid),
                        [ps_res[pB]], [sbt_res[bi]] + ssm_res["TC"] + ssm_res["TS"])
                    rel_ps(pga)
                    rel_ps(pgs)
                    rel_ps(pB)
                    DVE(lambda e, pA=pA, bi=bi: e.tensor_tensor(out=mg[:, bi, 0, :], in0=ps_t[pA][:, :], in1=sbt[:, bi, :],
                                                               op=ALU.mult),
                        [ps_res[pA], sbt_res[bi], actT_res] + ssm_res["TC"] + ssm_res["TS"], [mg_res[bi]])
                    rel_ps(pA)
                    POOL(lambda e, bi=bi: e.tensor_tensor(out=mg[:, bi, 0, :], in0=mg[:, bi, 0, :], in1=sg[:, bi, 1, :],
                                                          op=ALU.mult), [mg_res[bi], sg_res[bi], actT_res], [mg_res[bi]])
                    POOL(lambda e, bi=bi, j=j: e.tensor_tensor(out=mg[:, bi, 1, :], in0=attnT[:, j, :], in1=sg[:, bi, 0, :],
                                                               op=ALU.mult),
                         [attnT_res[j], sg_res[bi], mg_res[bi], actT_res], [mg_res[bi]])
                    DVE(lambda e, bi=bi, j=j: e.tensor_tensor(out=attnT[:, j, :], in0=mg[:, bi, 0, :], in1=mg[:, bi, 1, :],
                                                              op=ALU.add), [mg_res[bi], actT_res], [attnT_res[j]])
                done(r)
                done(rG)
            if first:
                for j in range(8):
                    dump("merged", attnT[:, j, :], [attnT_res[j]], idx=j)
            so = [get(U_WOUT), get(U_WOUT + 1)]
            for b in range(4):
                for half in range(2):
                    pi = alloc_ps()
                    for k in range(8):
                        slot, r = so[k // 4]
                        off = (k % 4) * 1024 + half * 512
                        mm(ps_t[pi][:, :], attnT[:, k, b * 128:(b + 1) * 128], wring[:, slot, off:off + 512], k == 0,
                           k == 7, [wring_res[slot], attnT_res[k]], [ps_res[pi]])
                    DVE(lambda e, pi=pi, b=b, half=half, hs=hs: e.tensor_tensor(
                        out=h2[:, hs, b, half * 512:(half + 1) * 512], in0=ps_t[pi][:, :],
                        in1=h2[:, hs, b, half * 512:(half + 1) * 512], op=ALU.add),
                        [ps_res[pi], h2_res[hs][b]], [h2_res[hs][b]])
                    rel_ps(pi)
            for slot, r in so:
                done(r)
            POOL(lambda e: e.tensor_copy(out=kk[:, :, :, 16:144], in_=kk[:, :, :, 16 + 4 * 128:16 + 5 * 128]),
                 [kk_res[5]], [kk_res[1]])
            POOL(lambda e: e.tensor_copy(out=vt[:, 0, :, :], in_=vt[:, 4, :, :]), [vt_res[4]], [vt_res[0]])
            if first:
                for b in range(4):
                    dump("h2", h2[:, hs, b, :], [h2_res[hs][b]], idx=b)

            def chain(hs=hs, s=s, t=t):
                yield from ffn_gen(1, hs, 4, 128)
                yield from final_gen(hs, s, t)
            prevF = chain()
        run_all(prevF)

        assert state["next"] == len(plan), (state["next"], len(plan))
        sc.finalize()
        sem_keys = [("E", e) for e in Sched.ENGS] + sorted(sc.dma_counts.keys())
        sems = {}
        for i, k in enumerate(sem_keys):
            sems[k] = es.enter_context(nc.semaphore("s%d" % i))
        block = es.enter_context(nc.Block())
        out_keys = sorted(sc.dma_counts.keys())

        @block.tensor
        def _(e):
            sc.emit("pe", e, sems)

        @block.scalar
        def _(e):
            sc.emit("act", e, sems)

        @block.vector
        def _(e):
            sc.emit("dve", e, sems)

        @block.gpsimd
        def _(e):
            sc.emit("pool", e, sems)
            for k in out_keys:
                e.wait_ge(sems[k], sc.dma_counts[k])

        @block.sync
        def _(e):
            sc.emit("sp", e, sems)
    return nc, sc


def _prep_shared(inp):
    f32 = np.float32
    g = lambda k: np.asarray(inp[k], dtype=f32)

    def w13_units(w1, w3):
        W = np.stack([w1, w3], 0).reshape(2, 8, 128, 11, 2, 128)
        return np.ascontiguousarray(W.transpose(3, 2, 4, 0, 1, 5)).reshape(11, 128, 4096)

    def w2_units(w2):
        W = np.zeros((24, 128, 1024), f32)
        W[:22] = w2.reshape(22, 128, 1024)
        return np.ascontiguousarray(W.reshape(6, 4, 128, 1024).transpose(0, 2, 1, 3)).reshape(6, 128, 4096)

    def win_units(w_in):
        q = w_in[:, 0:1024]
        k = w_in[:, 1024:1280]
        v = w_in[:, 1280:1536]
        u = w_in[:, 1536:2048]
        ga = w_in[:, 2048:3072]
        gs = w_in[:, 3072:4096]
        kdup = np.concatenate([k[:, (kh // 2) * 64:(kh // 2 + 1) * 64] for kh in range(8)], 1)
        vpad = np.concatenate([v, np.zeros((1024, 256), f32)], 1)
        cols = [kdup, vpad, q[:, :512], q[:, 512:], u]
        for m in range(4):
            cols.append(np.concatenate([ga[:, m * 256:(m + 1) * 256], gs[:, m * 256:(m + 1) * 256]], 1))
        return np.stack([np.ascontiguousarray(c.reshape(8, 128, 512).transpose(1, 0, 2)).reshape(128, 4096) for c in cols])

    wts = np.zeros((NUNIT, 128, 4096), f32)
    wts[0:11] = w13_units(g("ffn1_w1")[0], g("ffn1_w3")[0])
    wts[11:17] = w2_units(g("ffn1_w2")[0])
    wts[17:26] = win_units(g("w_in")[0])
    ga_ = g("ssm_glu_a")[0].reshape(4, 128, 1024)
    gb_ = g("ssm_glu_b")[0].reshape(4, 128, 1024)
    for m4 in range(4):
        blk = np.concatenate([ga_[:, :, m4 * 256:(m4 + 1) * 256], gb_[:, :, m4 * 256:(m4 + 1) * 256]], axis=2)
        wts[26 + m4, :, 0:2048] = np.ascontiguousarray(blk.transpose(1, 0, 2)).reshape(128, 2048)
    wts[30:32] = np.ascontiguousarray(g("w_out")[0].reshape(2, 4, 128, 1024).transpose(0, 2, 1, 3)).reshape(2, 128, 4096)
    wts[32:43] = w13_units(g("ffn2_w1")[0], g("ffn2_w3")[0])
    wts[43:49] = w2_units(g("ffn2_w2")[0])

    gam = np.zeros((128, 24), f32)
    for n, key in enumerate(("ffn1_norm", "mix_norm", "ffn2_norm")):
        gam[:, n * 8:(n + 1) * 8] = g(key)[0].reshape(8, 128).T
    gfin = np.ascontiguousarray(np.broadcast_to(g("final_norm")[None, :], (128, D)))

    a_re = g("ssm_a_re")[0]
    a_im = g("ssm_a_im")[0]
    lstep = g("ssm_log_step")[0]
    ssmp = np.zeros((128, 3, 16), f32)
    for p in range(16):
        for g2 in range(2):
            gi = 2 * p + g2
            ssmp[g2 * 64:(g2 + 1) * 64, 0, p] = a_re[gi]
            ssmp[g2 * 64:(g2 + 1) * 64, 1, p] = a_im[gi]
            ssmp[g2 * 64:(g2 + 1) * 64, 2, p] = lstep[gi]
    b_re = g("ssm_b_re")[0]
    b_im = g("ssm_b_im")[0]
    c_re = g("ssm_c_re")[0]
    c_im = g("ssm_c_im")[0]
    bl = np.zeros((128, 4, 2, 128), f32)
    cl = np.zeros((128, 16, 2, 32), f32)
    for p in range(16):
        ct, q = p // 4, p % 4
        for g2 in range(2):
            gi = 2 * p + g2
            rows = slice(32 * q + 16 * g2, 32 * q + 16 * g2 + 16)
            bl[rows, ct, 0, g2 * 64:(g2 + 1) * 64] = b_re[gi].T
            bl[rows, ct, 1, g2 * 64:(g2 + 1) * 64] = b_im[gi].T
            cl[g2 * 64:(g2 + 1) * 64, p, 0, g2 * 16:(g2 + 1) * 16] = c_re[gi].T
            cl[g2 * 64:(g2 + 1) * 64, p, 1, g2 * 16:(g2 + 1) * 16] = c_im[gi].T
    dcol = np.ascontiguousarray(g("ssm_d")[0].reshape(4, 128).T)
    sinks = g("attn_sinks")[0]
    sinkrow = np.zeros((1, 2048), f32)
    for kh in range(4):
        for half in range(2):
            for i in range(2):
                o = kh * 512 + half * 256 + i * 128
                sinkrow[0, o:o + 128] = sinks[4 * kh + 2 * i + half]
    kj = np.arange(128)[:, None]
    qi = np.arange(128)[None, :]
    maskp = np.where(kj > qi, 0.0, NEG).astype(f32)
    maskc = np.where(kj <= qi, 0.0, NEG).astype(f32)
    onehot = np.zeros((1, 32), f32)
    onehot[0, 16] = 1.0
    return {
        "meta": g("meta_tokens"), "wts": wts, "gam": gam, "gfin": gfin, "ssmp": ssmp, "bl": bl, "cl": cl,
        "dcol": dcol, "sinkrow": sinkrow, "ident": np.eye(128, dtype=f32),
        "maskp": np.tile(maskp, (1, 4)), "maskc": np.tile(maskc, (1, 4)), "onehot": onehot,
    }


_CACHE = {}


def kernel(**inputs):
    x = np.asarray(inputs["x"], dtype=np.float32)
    B, S, _ = x.shape
    ncores = 8
    nseq = B // ncores
    ntile = S // 512
    key = (nseq, ntile)
    if key not in _CACHE:
        _CACHE[key] = build(nseq, ntile)[0]
    nc = _CACHE[key]
    shared = _prep_shared(inputs)
    in_maps = []
    for c in range(ncores):
        m = dict(shared)
        m["x"] = np.ascontiguousarray(x[c * nseq:(c + 1) * nseq])
        in_maps.append(m)
    res = run_bass_kernel_spmd(nc, in_maps, core_ids=list(range(ncores)))
    return np.concatenate([r["y"] for r in res.results], axis=0)
```
